# Optimizing a Trainium2 kernel written in Bass

```python
import functools
import jax, jax.numpy as jnp
from jax import lax
import numpy as np

D_MODEL = 1024
BATCH = 8
SEQ = 2048
DEPTH = 1
DEC_BATCH = 128
DEC_SEQ = 4
PAST_LEN = 16384
PAGE_SIZE = 128

N_HEADS = 16
N_KV_HEADS = 2
HEAD_DIM = 64
GROUP = N_HEADS // N_KV_HEADS
WINDOW = 128
ROPE_THETA = 10000.0
CHUNK = 128
A_GROUPS = 4
A_GROUP_DIM = 128
A_WIDTH = A_GROUPS * A_GROUP_DIM
D_FF = 2816
LN_EPS = 1e-5
NEG_INF = -1e30
DN_ALPHA = (2.0 * DEPTH) ** 0.25
DN_BETA = (8.0 * DEPTH) ** -0.25
Q_WIDTH = N_HEADS * HEAD_DIM
KV_WIDTH = N_KV_HEADS * HEAD_DIM
SPLIT_POINTS = [A_WIDTH, 2 * A_WIDTH, 2 * A_WIDTH + Q_WIDTH, 2 * A_WIDTH + Q_WIDTH + KV_WIDTH,
                2 * A_WIDTH + Q_WIDTH + 2 * KV_WIDTH, 2 * A_WIDTH + Q_WIDTH + 2 * KV_WIDTH + D_MODEL]
IN_WIDTH = 2 * A_WIDTH + Q_WIDTH + 2 * KV_WIDTH + 2 * D_MODEL

kernel_name = 'gated_gmlp_swa_sink_macaron_deepnorm_step'


def _layernorm(x, g, b):
    xf = x.astype(jnp.float32)
    mu = xf.mean(-1, keepdims=True)
    var = jnp.square(xf - mu).mean(-1, keepdims=True)
    y = (xf - mu) * lax.rsqrt(var + LN_EPS) * g.astype(jnp.float32) + b.astype(jnp.float32)
    return y.astype(x.dtype)


def _swiglu(x, w_up, w_down):
    g, u = jnp.split(x @ w_up, 2, axis=-1)
    return (jax.nn.silu(g) * u) @ w_down


def _rope(x, pos):
    half = HEAD_DIM // 2
    inv = ROPE_THETA ** (-jnp.arange(half, dtype=jnp.float32) / half)
    ang = pos.astype(jnp.float32)[:, None] * inv[None, :]
    cos = jnp.cos(ang)[:, None, :]
    sin = jnp.sin(ang)[:, None, :]
    xf = x.astype(jnp.float32)
    x1, x2 = xf[..., :half], xf[..., half:]
    return jnp.concatenate([x1 * cos - x2 * sin, x2 * cos + x1 * sin], axis=-1).astype(x.dtype)


def _sink_attend(q, k, v, mask, sinks):
    s = jnp.einsum('...qkgd,...skd->...kgqs', q, k).astype(jnp.float32) * (HEAD_DIM ** -0.5)
    s = jnp.where(mask, s, NEG_INF)
    sk = sinks.astype(jnp.float32).reshape(N_KV_HEADS, GROUP, 1, 1)
    m = jnp.maximum(s.max(-1, keepdims=True), sk)
    p = jnp.exp(s - m)
    w = p / (p.sum(-1, keepdims=True) + jnp.exp(sk - m))
    return jnp.einsum('...kgqs,...skd->...qkgd', w.astype(v.dtype), v)


def _attend_prompt(q, k, v, sinks, buf):
    B, L = q.shape[0], q.shape[1]
    nb = L // WINDOW
    qb = q.reshape(B, nb, WINDOW, N_KV_HEADS, GROUP, HEAD_DIM)

    def two_blocks(t):
        tb = t.reshape(B, nb, WINDOW, N_KV_HEADS, HEAD_DIM)
        prev = jnp.pad(tb, ((0, 0), (1, 0), (0, 0), (0, 0), (0, 0)))[:, :nb]
        return jnp.concatenate([prev, tb], axis=2)

    i = jnp.arange(WINDOW)[:, None]
    j = jnp.arange(2 * WINDOW)[None, :]
    d = i + WINDOW - j
    band = (d >= 0) & (d < WINDOW)
    valid = (jnp.arange(nb) > 0)[:, None, None] | (j >= WINDOW)[None]
    mask = (band[None] & valid)[None, :, None, None]
    out = _sink_attend(qb, two_blocks(k), two_blocks(v), mask, sinks)
    return out.reshape(B, L, Q_WIDTH), k[:, L - buf:], v[:, L - buf:]


def _attend_sample(q, k, v, sinks, k_buf, v_buf):
    B, T = q.shape[0], q.shape[1]
    buf = k_buf.shape[1]
    kk = jnp.concatenate([k_buf.astype(k.dtype), k], axis=1)
    vv = jnp.concatenate([v_buf.astype(v.dtype), v], axis=1)
    qpos = PAST_LEN + jnp.arange(T)
    kpos = PAST_LEN - buf + jnp.arange(buf + T)
    d = qpos[:, None] - kpos[None, :]
    mask = (d >= 0) & (d < WINDOW)
    out = _sink_attend(q, kk, vv, mask, sinks)
    return out.reshape(B, T, Q_WIDTH), kk[:, T:], vv[:, T:]


def _chunk_gmlp(u, v, ln_g, ln_b, ws, bs):
    B, L = u.shape[0], u.shape[1]
    vn = _layernorm(v, ln_g, ln_b)
    nc = -(-L // CHUNK)
    lp = nc * CHUNK
    vc = jnp.pad(vn, ((0, 0), (0, lp - L), (0, 0))).reshape(B, nc, CHUNK, A_GROUPS, A_GROUP_DIM)
    causal = jnp.tril(jnp.ones((CHUNK, CHUNK), dtype=bool))
    wm = jnp.where(causal[None], ws, jnp.zeros_like(ws)).astype(vc.dtype)
    mixed = jnp.einsum('gts,bnsgc->bntgc', wm, vc) + bs.T[:, :, None].astype(vc.dtype)
    mixed = mixed.reshape(B, lp, A_WIDTH)[:, :L]
    start = ((L - 1) // CHUNK) * CHUNK
    return u * mixed, vn[:, start:]


def _layer(x, pos, attend, p):
    (ffn1_up, ffn1_down, ln1_g, ln1_b, w_in, a_ln_g, a_ln_b, a_ws, a_bs, attn_sinks,
     w_pa, w_pb, w_o, ln2_g, ln2_b, ffn2_up, ffn2_down, ln3_g, ln3_b) = p
    B, L = x.shape[0], x.shape[1]
    h = _layernorm(DN_ALPHA * x + 0.5 * _swiglu(x, ffn1_up, ffn1_down), ln1_g, ln1_b)
    a_u, a_v, q, k, v, g_a, g_b = jnp.split(h @ w_in, SPLIT_POINTS, axis=-1)
    a_out, a_state = _chunk_gmlp(jax.nn.gelu(a_u), jax.nn.gelu(a_v), a_ln_g, a_ln_b, a_ws, a_bs)
    q = _rope(q.reshape(B, L, N_HEADS, HEAD_DIM), pos).reshape(B, L, N_KV_HEADS, GROUP, HEAD_DIM)
    k = _rope(k.reshape(B, L, N_KV_HEADS, HEAD_DIM), pos)
    v = v.reshape(B, L, N_KV_HEADS, HEAD_DIM)
    b_out, k_state, v_state = attend(q, k, v, attn_sinks)
    merged = jax.nn.sigmoid(g_a) * (a_out @ w_pa) + jax.nn.sigmoid(g_b) * (b_out @ w_pb)
    h = _layernorm(DN_ALPHA * h + merged @ w_o, ln2_g, ln2_b)
    h = _layernorm(DN_ALPHA * h + 0.5 * _swiglu(h, ffn2_up, ffn2_down), ln3_g, ln3_b)
    return h, k_state, v_state, a_state


def setup_inputs(seed: int = 0) -> dict:
    key = jax.random.key(seed)
    ks = jax.random.split(key, 32)
    f32 = jnp.float32
    buf = min(WINDOW, PAST_LEN)

    def nrm(k, shape, scale):
        return jax.random.normal(k, shape, f32) * scale

    def gain(k, n):
        return 1.0 + nrm(k, (DEPTH, n), 0.02)

    return {
        'x_prompt': nrm(ks[0], (BATCH, SEQ, D_MODEL), 1.0),
        'x_sample': nrm(ks[1], (DEC_BATCH, DEC_SEQ, D_MODEL), 1.0),
        'cache_win_k': nrm(ks[2], (DEPTH, DEC_BATCH, buf, N_KV_HEADS, HEAD_DIM), 1.0),
        'cache_win_v': nrm(ks[3], (DEPTH, DEC_BATCH, buf, N_KV_HEADS, HEAD_DIM), 1.0),
        'ffn1_up': nrm(ks[4], (DEPTH, D_MODEL, 2 * D_FF), D_MODEL ** -0.5),
        'ffn1_down': nrm(ks[5], (DEPTH, D_FF, D_MODEL), DN_BETA * D_FF ** -0.5),
        'ln1_g': gain(ks[6], D_MODEL),
        'ln1_b': nrm(ks[7], (DEPTH, D_MODEL), 0.02),
        'w_in': nrm(ks[8], (DEPTH, D_MODEL, IN_WIDTH), D_MODEL ** -0.5),
        'a_ln_g': gain(ks[9], A_WIDTH),
        'a_ln_b': nrm(ks[10], (DEPTH, A_WIDTH), 0.02),
        'a_ws': nrm(ks[11], (DEPTH, A_GROUPS, CHUNK, CHUNK), CHUNK ** -0.5),
        'a_bs': 1.0 + nrm(ks[12], (DEPTH, A_GROUPS, CHUNK), 0.02),
        'attn_sinks': nrm(ks[13], (DEPTH, N_HEADS), 0.5),
        'w_pa': nrm(ks[14], (DEPTH, A_WIDTH, D_MODEL), A_WIDTH ** -0.5),
        'w_pb': nrm(ks[15], (DEPTH, Q_WIDTH, D_MODEL), Q_WIDTH ** -0.5),
        'w_o': nrm(ks[16], (DEPTH, D_MODEL, D_MODEL), DN_BETA * D_MODEL ** -0.5),
        'ln2_g': gain(ks[17], D_MODEL),
        'ln2_b': nrm(ks[18], (DEPTH, D_MODEL), 0.02),
        'ffn2_up': nrm(ks[19], (DEPTH, D_MODEL, 2 * D_FF), D_MODEL ** -0.5),
        'ffn2_down': nrm(ks[20], (DEPTH, D_FF, D_MODEL), DN_BETA * D_FF ** -0.5),
        'ln3_g': gain(ks[21], D_MODEL),
        'ln3_b': nrm(ks[22], (DEPTH, D_MODEL), 0.02),
    }


def reference(x_prompt, x_sample, cache_win_k, cache_win_v, ffn1_up, ffn1_down, ln1_g, ln1_b,
              w_in, a_ln_g, a_ln_b, a_ws, a_bs, attn_sinks, w_pa, w_pb, w_o, ln2_g, ln2_b,
              ffn2_up, ffn2_down, ln3_g, ln3_b):
    buf = cache_win_k.shape[2]
    weights = (ffn1_up, ffn1_down, ln1_g, ln1_b, w_in, a_ln_g, a_ln_b, a_ws, a_bs, attn_sinks,
               w_pa, w_pb, w_o, ln2_g, ln2_b, ffn2_up, ffn2_down, ln3_g, ln3_b)
    pos_p = jnp.arange(x_prompt.shape[1])
    pos_s = PAST_LEN + jnp.arange(x_sample.shape[1])
    hp, hs = x_prompt, x_sample
    kp_l, vp_l, ks_l, vs_l, ap_l, as_l = [], [], [], [], [], []
    for l in range(DEPTH):
        p = tuple(w[l] for w in weights)
        hp, kp, vp, ap = _layer(hp, pos_p, functools.partial(_attend_prompt, buf=buf), p)
        hs, ks_, vs_, as_ = _layer(hs, pos_s, functools.partial(_attend_sample, k_buf=cache_win_k[l], v_buf=cache_win_v[l]), p)
        kp_l.append(kp); vp_l.append(vp); ap_l.append(ap)
        ks_l.append(ks_); vs_l.append(vs_); as_l.append(as_)
    win_k_prompt = jnp.stack(kp_l)
    win_v_prompt = jnp.stack(vp_l)
    win_k_sample = jnp.stack(ks_l)
    win_v_sample = jnp.stack(vs_l)
    chunk_v_prompt = jnp.stack(ap_l)
    chunk_v_sample = jnp.stack(as_l)
    return (hp, hs, win_k_prompt, win_v_prompt, win_k_sample, win_v_sample, chunk_v_prompt, chunk_v_sample)
```

```python
import contextlib
from concourse.bass_utils import run_bass_kernel_spmd
import numpy as np
import concourse.bass as bass
import concourse.mybir as mybir

F32 = mybir.dt.float32
BF16 = mybir.dt.bfloat16
AF = mybir.ActivationFunctionType
ALU = mybir.AluOpType
AX = mybir.AxisListType

ENGS = ['pe', 'act', 'dve', 'pool', 'sp']


class Sched:
    def __init__(self):
        self.ops = {e: [] for e in ENGS}
        self.w = {}
        self.r = {}
        self.dma_count = {}
        self.extra = {}

    def frontier(self):
        f = {}
        for e in ENGS:
            if e == 'sp':
                continue
            if self.ops[e]:
                for i in range(len(self.ops[e]) - 1, -1, -1):
                    if self.ops[e][i]['tok'][0] == 'c':
                        f[('c', e)] = self.ops[e][i]['tok']
                        break
        for k, n in self.dma_count.items():
            f[('d', k)] = ('d', k, n)
        return f

    def add(self, eng, fn, reads=(), writes=(), dma=None, extra=None):
        deps = set()
        for k in reads:
            deps |= set(self.w.get(k, {}).values())
        for k in writes:
            deps |= set(self.w.get(k, {}).values())
            deps |= set(self.r.get(k, {}).values())
        if extra:
            deps |= set(extra.values())
        idx = len(self.ops[eng])
        if dma is not None:
            seq = self.dma_count.get(dma, 0) + 1
            self.dma_count[dma] = seq
            tok = ('d', dma, seq)
            src = ('d', dma)
        else:
            tok = ('c', eng, idx)
            src = ('c', eng)
        if eng == 'pe':
            deps = {d for d in deps if not (d[0] == 'c' and d[1] == 'pe')}
        deps.discard(tok)
        op = dict(eng=eng, fn=fn, deps=deps, tok=tok, needed=False)
        self.ops[eng].append(op)
        for d in deps:
            if d[0] == 'c':
                self.ops[d[1]][d[2]]['needed'] = True
        for k in reads:
            self.r.setdefault(k, {})[src] = tok
        for k in writes:
            self.w.setdefault(k, {})[src] = tok
        return tok

    def emit(self, nc, final_wait_eng='sp'):
        import contextlib
        semval = {}
        for e in ENGS:
            c = 0
            for i, op in enumerate(self.ops[e]):
                if op['tok'][0] == 'c' and op['needed']:
                    c += 1
                    semval[op['tok']] = c
        dma_keys = list(self.dma_count.keys())
        with nc.cleanup_on_exit():
          with contextlib.ExitStack() as st:
            csem = {e: nc.alloc_semaphore(name='c_' + e) for e in ENGS}
            dsem = {k: nc.alloc_semaphore(name='d%d' % i) for i, k in enumerate(dma_keys)}
            block = st.enter_context(nc.Block())

            def run(eng_name, eobj):
                waited = {}
                for op in self.ops[eng_name]:
                    need = {}
                    for d in op['deps']:
                        if d[0] == 'c':
                            sem = csem[d[1]]
                            val = semval[d]
                            kk = ('c', d[1])
                        else:
                            sem = dsem[d[1]]
                            val = 16 * d[2]
                            kk = ('d', d[1])
                        if need.get(kk, (None, 0))[1] < val:
                            need[kk] = (sem, val)
                    for kk, (sem, val) in need.items():
                        if waited.get(kk, 0) < val:
                            eobj.wait_ge(sem, val)
                            waited[kk] = val
                    inst = op['fn'](eobj)
                    if op['tok'][0] == 'd':
                        inst.then_inc(dsem[op['tok'][1]], 16)
                    elif op['needed']:
                        inst.then_inc(csem[eng_name], 1)
                if eng_name == final_wait_eng:
                    for k, n in self.dma_count.items():
                        if waited.get(('d', k), 0) < 16 * n:
                            eobj.wait_ge(dsem[k], 16 * n)

            @block.tensor
            def _(e):
                run('pe', e)

            @block.scalar
            def _(e):
                run('act', e)

            @block.vector
            def _(e):
                run('dve', e)

            @block.gpsimd
            def _(e):
                run('pool', e)

            @block.sync
            def _(e):
                run('sp', e)
          nc.all_engine_barrier()


class Arena:
    def __init__(self, t, nbytes):
        self.t = t
        self.nbytes = nbytes
        self.off = 0
        self.peak = 0

    def alloc(self, nfree, dtype, parts=128):
        esz = 2 if dtype == BF16 else 4
        nb = (nfree * esz + 31) // 32 * 32
        assert self.off + nb <= self.nbytes, ("arena overflow", self.off, nb, self.nbytes)
        a = self.t[0:parts, self.off // 4:(self.off + nb) // 4]
        if dtype != F32:
            a = a.bitcast(dtype)
        a = a[:, 0:nfree]
        self.off += nb
        self.peak = max(self.peak, self.off)
        return a

    def mark(self):
        return self.off

    def reset(self, m):
        self.off = m


D_MODEL = 1024; SEQ = 2048; DEC_SEQ = 4; PAST_LEN = 16384
N_HEADS = 16; N_KV = 2; HD = 64; D_FF = 2816; NJ = 22
LN_EPS = 1e-5
ALPHA = 2.0 ** 0.25
SCALE = HD ** -0.5
NEG = -30000.0
NCORES = 8
SB = 16


def build_program(do_prompt=True, n_pblocks=4):
    nc = bass.Bass("TRN2", target_bir_lowering=False)

    def din(name, shape):
        return nc.dram_tensor(name, list(shape), F32, kind="ExternalInput").ap()

    def dout(name, shape):
        return nc.dram_tensor(name, list(shape), F32, kind="ExternalOutput").ap()

    x_p = din("x_p", [SEQ, 1024]); x_s = din("x_s", [64, 1024])
    ck = din("ck", [SB, 128, 128]); cv = din("cv", [SB, 128, 128])
    Wd = dict(
        ffn1_up=din("ffn1_up", [1024, 5632]), ffn1_down=din("ffn1_down", [2816, 1024]),
        w_in=din("w_in", [1024, 4352]), w_pa=din("w_pa", [512, 1024]), w_pb=din("w_pb", [1024, 1024]),
        w_o=din("w_o", [1024, 1024]), ffn2_up=din("ffn2_up", [1024, 5632]), ffn2_down=din("ffn2_down", [2816, 1024]))
    d_lnc = din("lnc", [128, 6 * 1024]); d_alnc = din("alnc", [128, 1024]); d_lncol = din("lncol", [128, 48])
    d_wsTp = din("wsT_p", [128, 512]); d_biasp = din("bias_p", [128, 512])
    d_wsTs = din("wsT_s", [64, 256]); d_biass = din("bias_s", [128, 256])
    d_sinks = din("sinks", [128, 16]); d_sinks2 = din("sinks2", [128, 8])
    d_ident = din("ident", [128, 128]); d_triu = din("triu", [128, 128]); d_bdm = din("bdm", [64, 64])
    d_ropep = din("rope_p", [128, 1024]); d_ropes = din("rope_s", [64, 64])
    d_maskp = din("mask_p", [128, 256]); d_masks = din("mask_s", [128, 2112])

    y_p = dout("y_p", [SEQ, 1024]); y_s = dout("y_s", [64, 1024])
    wkp = dout("wkp", [128, 128]); wvp = dout("wvp", [128, 128])
    wks = dout("wks", [SB, 128, 128]); wvs = dout("wvs", [SB, 128, 128])
    cvp = dout("cvp", [128, 512]); cvs = dout("cvs", [64, 512])

    S = Sched()
    st = contextlib.ExitStack()
    ARENA_B = 206 * 1024
    art = st.enter_context(nc.sbuf_tensor("arena", [128, ARENA_B // 4], F32))
    ps = st.enter_context(nc.psum_tensor("ps", [128, 4096], F32))
    A = Arena(art, ARENA_B)

    def pb(b):
        return ps[:, 512 * b:512 * (b + 1)]

    def pbb(b):
        return ps[:, 512 * b:512 * (b + 1)].bitcast(BF16)

    bank_i = [0]

    def nb():
        b = bank_i[0]
        bank_i[0] = (b + 1) % 8
        return b

    cur_extra = [None]
    alias_keys = set()

    def add(eng, fn, reads=(), writes=(), dma=None):
        ex = None
        if cur_extra[0] is not None:
            ex = cur_extra[0]
        return S.add(eng, fn, reads=list(reads), writes=list(writes), dma=dma, extra=ex)

    def mm(out, lhsT, rhs, start, stop, reads, writes):
        add('pe', lambda e: e.matmul(out, lhsT=lhsT, rhs=rhs, start=start, stop=stop), reads, writes)

    def tr(out, in_, idn, reads, writes):
        add('pe', lambda e: e.transpose(out, in_, idn), reads, writes)

    def actf(out, in_, func, reads, writes, bias=None, scale=None, eng='act'):
        kw = {}
        if bias is not None:
            kw['bias'] = bias
        if scale is not None:
            kw['scale'] = scale
        add(eng, lambda e: e.activation(out=out, in_=in_, func=func, **kw), reads, writes)

    def tt(eng, out, in0, in1, op, reads, writes):
        add(eng, lambda e: e.tensor_tensor(out=out, in0=in0, in1=in1, op=op), reads, writes)

    def tsc(eng, out, in0, s1, s2, op0, op1, reads, writes):
        if op1 is None:
            add(eng, lambda e: e.tensor_scalar(out=out, in0=in0, scalar1=s1, scalar2=None, op0=op0), reads, writes)
        else:
            add(eng, lambda e: e.tensor_scalar(out=out, in0=in0, scalar1=s1, scalar2=s2, op0=op0, op1=op1), reads, writes)

    def stt(eng, out, in0, scalar, in1, op0, op1, reads, writes):
        add(eng, lambda e: e.scalar_tensor_tensor(out=out, in0=in0, scalar=scalar, in1=in1, op0=op0, op1=op1), reads, writes)

    def cp(eng, out, in_, reads, writes):
        if eng == 'act':
            add(eng, lambda e: e.activation(out=out, in_=in_, func=AF.Copy), reads, writes)
        else:
            add(eng, lambda e: e.tensor_copy(out=out, in_=in_), reads, writes)

    def dma(out, in_, reads, writes, key, eng='sp'):
        add(eng, lambda e: e.dma_start(out=out, in_=in_), reads, writes, dma=key)

    ident = A.alloc(128, BF16)
    ident32 = A.alloc(128, F32)
    lncol = A.alloc(48, F32).rearrange("p (a k) -> p a k", a=6)
    lnc = A.alloc(6 * 1024, F32).rearrange("p (a n) -> p a n", a=6)
    alnc = A.alloc(1024, F32).rearrange("p (a n) -> p a n", a=2)
    WmTp = A.alloc(512, BF16).rearrange("p (g t) -> p g t", g=4)
    biasp = A.alloc(512, F32).rearrange("p (g t) -> p g t", g=4)
    WmTs = A.alloc(256, BF16).rearrange("p (g t) -> p g t", g=4)
    biass = A.alloc(256, F32).rearrange("p (g t) -> p g t", g=4)
    sinks = A.alloc(16, F32); negsinks = A.alloc(16, F32)
    sinks2 = A.alloc(8, F32); negsinks2 = A.alloc(8, F32)
    ropep = A.alloc(1024, F32).rearrange("p (c t f) -> p c t f", c=2, t=16)
    ropes = A.alloc(64, F32).rearrange("p (c f) -> p c f", c=2)
    maskp = A.alloc(256, BF16); masks = A.alloc(2112, BF16)
    mhalf = A.alloc(1, F32)
    stage = [A.alloc(1024, F32) for _ in range(4)]
    NR = 8
    ring = [A.alloc(2048, BF16) for _ in range(NR)]
    res = A.alloc(4 * 1024, F32).rearrange("p (t n) -> p t n", t=4)
    actT = A.alloc(8 * 512, BF16).rearrange("p (k t) -> p k t", k=8)
    KT = A.alloc(2 * 2112, BF16).rearrange("p (k t) -> p k t", k=2)
    Vaug = A.alloc(16 * 2 * 66, BF16).rearrange("p (b k d) -> p b k d", b=16, k=2)
    Vn = A.alloc(2 * 66, BF16).rearrange("p (k d) -> p k d", k=2)
    ybuf = [A.alloc(1024, F32) for _ in range(1)]
    xb = [A.alloc(1024, BF16) for _ in range(4)]
    lnst = [A.alloc(16, F32) for _ in range(5)]
    lnst2 = [A.alloc(8, F32) for _ in range(4)]
    m_alias = A.mark()
    tmpc = A.alloc(5632, F32)
    A.reset(m_alias)
    hT = A.alloc(NJ * 512, BF16).rearrange("p (j t) -> p j t", j=NJ)
    sg = [A.alloc(512, F32) for _ in range(2)]
    A.reset(m_alias)
    guT = A.alloc(4 * 128, F32).rearrange("p (g t) -> p g t", g=4)
    a_outT = A.alloc(4 * 512, BF16).rearrange("p (g t) -> p g t", g=4)
    gv = A.alloc(512, F32); vn = A.alloc(512, F32); vnb = A.alloc(512, BF16)
    mixtmp = A.alloc(512, F32).rearrange("p (g t) -> p g t", g=4)
    b_outT = A.alloc(8 * 512, BF16).rearrange("p (k t) -> p k t", k=8)
    sig = [A.alloc(512, F32) for _ in range(2)]
    t1 = A.alloc(4 * 512, F32).rearrange("p (t n) -> p t n", t=4)
    t2 = A.alloc(512, F32)
    m_e = A.mark()
    q_rot = A.alloc(1024, BF16).rearrange("p (h d) -> p h d", h=16)
    qT_raw = A.alloc(16 * 128, BF16)
    qT = qT_raw.rearrange("p (h t) -> p h t", h=16)
    qTs = qT_raw[:, 0:1024].rearrange("p (h t) -> p h t", h=16)
    kf = A.alloc(128, F32); kbf = A.alloc(128, BF16); vf = A.alloc(128, F32)
    rt = [A.alloc(128, F32) for _ in range(4)]
    Pb = [A.alloc(512, BF16) for _ in range(3)]
    PT = [A.alloc(512, BF16) for _ in range(3)]
    Ps = A.alloc(2112, BF16)
    PTs = A.alloc(17 * 128, BF16).rearrange("p (b t) -> p b t", b=17)
    tmpO = A.alloc(64, BF16)
    b_out = A.alloc(1024, BF16).rearrange("p (h d) -> p h d", h=16)
    ckb = A.alloc(2048, BF16)
    mx5 = A.alloc(8, F32)
    mx5s = [mx5, A.alloc(8, F32)]
    sm = {n: A.alloc(16, F32) for n in ['mx', 'negm', 'es', 'den', 'rden', 'tmp2']}
    A.reset(m_e)
    mergedb = A.alloc(4 * 1024, BF16).rearrange("p (t n) -> p t n", t=4)
    print("arena peak bytes", A.peak)

    def load_const(dst, src, key, parts=128):
        dma(dst, src, [], [key], key=('c', key), eng='act')

    load_const(lnc.rearrange("p a n -> p (a n)"), d_lnc, 'lnc')
    load_const(alnc.rearrange("p a n -> p (a n)"), d_alnc, 'alnc')
    load_const(lncol.rearrange("p a k -> p (a k)"), d_lncol, 'lncol')
    load_const(ident32, d_ident, 'ident32')
    load_const(biasp.rearrange("p g t -> p (g t)"), d_biasp, 'biasp')
    load_const(biass.rearrange("p g t -> p (g t)"), d_biass, 'biass')
    load_const(sinks, d_sinks, 'sinks')
    load_const(sinks2, d_sinks2, 'sinks2')
    load_const(ropep.rearrange("p c t f -> p (c t f)"), d_ropep, 'ropep')
    load_const(ropes[0:64].rearrange("p c f -> p (c f)"), d_ropes, 'ropes')
    add('pool', lambda e: e.memset(mhalf, -0.5), [], ['mhalf'])
    tsc('dve', negsinks, sinks, -1.0, None, ALU.mult, None, ['sinks'], ['negsinks'])
    tsc('dve', negsinks2, sinks2, -1.0, None, ALU.mult, None, ['sinks2'], ['negsinks2'])
    t_id = tmpc[:, 0:128]; t_ws = tmpc[:, 128:640]; t_tri = tmpc[:, 640:768]
    t_wss = tmpc[0:64, 768:1024]; t_bd = tmpc[0:64, 1024:1088]; t_mp = tmpc[:, 1088:1344]; t_ms = tmpc[:, 1344:3456]
    dma(t_id, d_ident, [], ['t_id'], key=('c', 't_id'), eng='act')
    cp('dve', ident, t_id, ['t_id'], ['ident'])
    dma(t_ws, d_wsTp, [], ['t_ws'], key=('c', 't_ws'), eng='act')
    dma(t_tri, d_triu, [], ['t_tri'], key=('c', 't_tri'), eng='act')
    tt('dve', WmTp, t_ws.rearrange("p (g t) -> p g t", g=4), t_tri.unsqueeze(1).broadcast_to([128, 4, 128]), ALU.mult,
       ['t_ws', 't_tri'], ['WmTp'])
    dma(t_wss, d_wsTs, [], ['t_wss'], key=('c', 't_wss'), eng='act')
    dma(t_bd, d_bdm, [], ['t_bd'], key=('c', 't_bd'), eng='act')
    tt('dve', WmTs[0:64], t_wss.rearrange("p (g t) -> p g t", g=4), t_bd.unsqueeze(1).broadcast_to([64, 4, 64]), ALU.mult,
       ['t_wss', 't_bd'], ['WmTs'])
    dma(t_mp, d_maskp, [], ['t_mp'], key=('c', 't_mp'), eng='act')
    cp('dve', maskp, t_mp, ['t_mp'], ['maskp'])
    dma(t_ms, d_masks, [], ['t_ms'], key=('c', 't_ms'), eng='act')
    cp('dve', masks, t_ms, ['t_ms'], ['masks'])
    add('pool', lambda e: e.memset(Vaug.rearrange("p b k d -> p (b k d)"), 1.0), [], ['Vaug'])
    add('pool', lambda e: e.memset(Vn.rearrange("p k d -> p (k d)"), 1.0), [], ['Vn'])

    def wview(name):
        w = Wd[name]
        if name.endswith('_up') or name == 'w_in' or name in ('w_pb', 'w_o'):
            return w.rearrange("(kc p) n -> p kc n", p=128)
        return w.rearrange("(j p) n -> p j n", p=128)

    def unit_list(kind):
        groups = [0]
        L = []
        for f in ('ffn1', 'ffn2'):
            sub = []
            up = wview(f + '_up'); dn = wview(f + '_down')
            for i in range(11):
                sub.append((f + 'g%d' % i, up[:, :, 256 * i:256 * i + 256], (8, 256)))
                sub.append((f + 'u%d' % i, up[:, :, 2816 + 256 * i:2816 + 256 * i + 256], (8, 256)))
            for g in groups:
                for i in range(11):
                    sub.append((f + 'd%d_%d' % (g, i), dn[:, 2 * i:2 * i + 2, :], (2, 1024)))
            if f == 'ffn1':
                L += sub
                wi = wview('w_in')
                for c in range(2):
                    L.append(('au%d' % c, wi[:, :, 256 * c:256 * c + 256], (8, 256)))
                for c in range(2):
                    L.append(('av%d' % c, wi[:, :, 512 + 256 * c:512 + 256 * c + 256], (8, 256)))
                for c in range(4):
                    L.append(('q%d' % c, wi[:, :, 1024 + 256 * c:1024 + 256 * c + 256], (8, 256)))
                L.append(('kv', wi[:, :, 2048:2304], (8, 256)))
                pa = wview('w_pa'); pbw = wview('w_pb'); wo = wview('w_o')
                for n in range(2):
                    L.append(('pa%d' % n, pa[:, :, 512 * n:512 * n + 512], (4, 512)))
                    L.append(('ga%d' % (2 * n), wi[:, :, 2304 + 512 * n:2304 + 512 * n + 256], (8, 256)))
                    L.append(('ga%d' % (2 * n + 1), wi[:, :, 2304 + 512 * n + 256:2304 + 512 * n + 512], (8, 256)))
                    L.append(('pb0_%d' % n, pbw[:, 0:4, 512 * n:512 * n + 512], (4, 512)))
                    L.append(('pb1_%d' % n, pbw[:, 4:8, 512 * n:512 * n + 512], (4, 512)))
                    L.append(('gb%d' % (2 * n), wi[:, :, 3328 + 512 * n:3328 + 512 * n + 256], (8, 256)))
                    L.append(('gb%d' % (2 * n + 1), wi[:, :, 3328 + 512 * n + 256:3328 + 512 * n + 512], (8, 256)))
                for n in range(2):
                    L.append(('wo0_%d' % n, wo[:, 0:4, 512 * n:512 * n + 512], (4, 512)))
                    L.append(('wo1_%d' % n, wo[:, 4:8, 512 * n:512 * n + 512], (4, 512)))
            else:
                L += sub
        return L

    import re as _re

    def canon(name):
        return _re.sub(r"d\d_(\d+)$", r"d_\1", name)

    all_units = []
    blocks = ([('p', b) for b in range(n_pblocks)] if do_prompt else []) + [('s', 0)]
    for bidx, (kind, b) in enumerate(blocks):
        for (name, src, shp) in unit_list(kind):
            all_units.append((name, src, shp, bidx))
    cids = {}
    for (name, src, shp, bidx) in all_units:
        if bidx == 0:
            cids.setdefault(canon(name), len(cids))
    wscr = nc.dram_tensor("wscr", [len(cids), 128, 2048], BF16).ap()

    CONV_SPLIT = False

    class WS:
        cur = 0
        issued = 0
        released = 0
        DMAX = 7
        stg_i = 0

    def w_issue_until():
        while WS.issued < len(all_units) and WS.issued <= WS.cur + WS.DMAX and WS.issued - NR < WS.released:
            k = WS.issued
            name, src, (a, b), bidx = all_units[k]
            ri = k % NR
            ci = cids[canon(name)]
            is_ffn = name.startswith('ffn')
            convert = (bidx == 0) or (bidx == 1 and is_ffn and CONV_SPLIT)
            wb = (bidx == 0 and (not is_ffn or not CONV_SPLIT)) or (bidx == 1 and is_ffn and CONV_SPLIT)
            if convert:
                for hf in range(2):
                    si = WS.stg_i % len(stage); WS.stg_i += 1
                    a2 = a // 2
                    sv = stage[si].rearrange("p (a b) -> p a b", a=a2)
                    dma(sv, src[:, hf * a2:(hf + 1) * a2, :], [], [('stg', si)], key=('stg', si))
                    ce = ('act', 'dve')[WS.stg_i % 2]
                    cp(ce, ring[ri][:, hf * 1024:(hf + 1) * 1024], stage[si], [('stg', si)], [('ring', ri)])
                if wb:
                    dma(wscr[ci], ring[ri], [('ring', ri)], [('scr', ci)], key=('wb', ri), eng='pool')
            else:
                dma(ring[ri], wscr[ci], [('scr', ci)], [('ring', ri)], key=('ring', ri))
            WS.issued += 1

    def w_next(expect):
        k = WS.cur
        name, src, (a, b), bidx = all_units[k]
        assert name == expect, (name, expect)
        WS.cur += 1
        w_issue_until()
        assert WS.issued > k, ("weight stream stalled", name, WS.issued, WS.released)
        return ring[k % NR].rearrange("p (a b) -> p a b", a=a), ('ring', k % NR)

    def w_release(n=1):
        WS.released += n
        w_issue_until()

    def layernorm(yi, R, width, g_ap, b_ap, eps, out_ap, out_keys, heavy='dve', ybuf_ap=None, ykey=None):
        y = ybuf[yi] if ybuf_ap is None else ybuf_ap
        yk = ('y', yi) if ykey is None else ykey
        sk = ('lnst', yi)
        stt_ = lnst[yi]
        nch = width // 512
        for c in range(nch):
            add('dve', (lambda c: lambda e: e.bn_stats(out=stt_[0:R, 6 * c:6 * c + 6], in_=y[0:R, 512 * c:512 * c + 512]))(c), [yk], [sk])
        add('dve', lambda e: e.bn_aggr(out=stt_[0:R, 12:14], in_=stt_[0:R, 0:6 * nch]), [sk], [sk])
        tsc('dve', stt_[0:R, 14:15], stt_[0:R, 13:14], eps, None, ALU.add, None, [sk], [sk])
        tt('pool', stt_[0:R, 15:16], stt_[0:R, 14:15], mhalf[0:R, 0:1], ALU.pow, [sk, 'mhalf'], [sk])
        if heavy == 'dve':
            tsc('dve', y[0:R, 0:width], y[0:R, 0:width], stt_[0:R, 12:13], stt_[0:R, 15:16], ALU.subtract, ALU.mult, [yk, sk], [yk])
            tt('dve', y[0:R, 0:width], y[0:R, 0:width], g_ap, ALU.mult, [yk, 'lnc', 'alnc'], [yk])
            tt('dve', out_ap, y[0:R, 0:width], b_ap, ALU.add, [yk, 'lnc', 'alnc'], out_keys)
        else:
            sk2 = ('lnst2', yi)
            tt('pool', lnst2[yi][0:R, 0:1], stt_[0:R, 12:13], stt_[0:R, 15:16], ALU.mult, [sk], [sk2])
            tsc('pool', lnst2[yi][0:R, 0:1], lnst2[yi][0:R, 0:1], -1.0, None, ALU.mult, None, [sk2], [sk2])
            tsc('pool', y[0:R, 0:width], y[0:R, 0:width], stt_[0:R, 15:16], lnst2[yi][0:R, 0:1], ALU.mult, ALU.add, [yk, sk, sk2], [yk])
            tt('pool', y[0:R, 0:width], y[0:R, 0:width], g_ap, ALU.mult, [yk, 'lnc', 'alnc'], [yk])
            tt('pool', out_ap, y[0:R, 0:width], b_ap, ALU.add, [yk, 'lnc', 'alnc'], out_keys)

    def xb_fill(ti, R, src_ap=None, src_keys=None, cast_eng='act', dram_src=None):
        xi = ti % 4
        if dram_src is not None:
            dma(xb[xi][0:R, :], dram_src, [], [('xb', xi)], key=('xb', xi), eng='pool')
        else:
            cp(cast_eng, xb[xi][0:R, :], src_ap, src_keys, [('xb', xi)])

    def xb_transpose(ti, R, c0):
        xi = ti % 4
        b = nb()
        for kc in range(8):
            tr(pbb(b)[:, kc * 128:kc * 128 + R], xb[xi][0:R, kc * 128:(kc + 1) * 128], ident[0:R, 0:R],
               [('xb', xi), 'ident'], [('ps', b)])
        cp('act', actT[:, :, c0:c0 + R], pbb(b).rearrange("p (k t) -> p k t", k=8)[:, :, 0:R], [('ps', b)], [('actT', ti)])

    def to_actT(src_ap, src_keys, R, c0, ti, cast_eng='act'):
        xb_fill(ti, R, src_ap, src_keys, cast_eng)
        xb_transpose(ti, R, c0)

    def run_pipeline(NT, stages):
        minlag = min(l for l, _ in stages)
        stages = [(l - minlag, f) for l, f in stages]
        maxlag = max(l for l, _ in stages)
        order = sorted(stages, key=lambda t: -t[0])
        for step in range(NT + maxlag):
            for lag, fn in order:
                if 0 <= step - lag < NT:
                    fn(step - lag)

    def ln_res_pipeline(tiles, ln_g, ln_b, eps, pre=None, extra=None, ln_i=0):
        NT = len(tiles)

        def s1(ti):
            R = tiles[ti]; rk = ('res', ti); sk = ('lnst', ti % 4); st_ = lnst[ti % 4]
            if pre is not None:
                pre(ti)
            y = res[:, ti, :]
            for c in range(2):
                add('dve', (lambda c=c: lambda e: e.bn_stats(out=st_[0:R, 6 * c:6 * c + 6], in_=y[0:R, 512 * c:512 * c + 512]))(), [rk], [sk])
            add('dve', lambda e: e.bn_aggr(out=st_[0:R, 12:14], in_=st_[0:R, 0:12]), [sk], [sk])
            tsc('dve', st_[0:R, 14:15], st_[0:R, 13:14], eps, None, ALU.add, None, [sk], [sk])
            tt('pool', st_[0:R, 15:16], st_[0:R, 14:15], mhalf[0:R, 0:1], ALU.pow, [sk, 'mhalf'], [sk])

        def s2(ti):
            R = tiles[ti]; rk = ('res', ti); sk = ('lnst', ti % 4); st_ = lnst[ti % 4]
            y = res[0:R, ti, :]
            tsc('dve', y, y, st_[0:R, 12:13], st_[0:R, 15:16], ALU.subtract, ALU.mult, [rk, sk], [rk])

        def s3(ti):
            R = tiles[ti]; rk = ('res', ti); c0 = 128 * ti
            bA = nb(); bB = nb()
            for kc in range(8):
                bk = bA if kc < 4 else bB
                tr(pb(bk)[:, (kc % 4) * 128:(kc % 4) * 128 + R], res[0:R, ti, kc * 128:(kc + 1) * 128], ident32[0:R, 0:R],
                   [rk, 'ident32'], [('ps', bk)])
            for kc in range(8):
                bk = bA if kc < 4 else bB
                actf(actT[:, kc, c0:c0 + R], pb(bk)[:, (kc % 4) * 128:(kc % 4) * 128 + R], AF.Identity, [('ps', bk), 'lncol'], [('actT', ti)],
                     bias=lncol[:, ln_i + 1, kc:kc + 1], scale=lncol[:, ln_i, kc:kc + 1])
            y = res[0:R, ti, :]
            tt('pool', y, y, ln_g[0:R, :], ALU.mult, [rk, 'lnc'], [rk])
            tt('pool', y, y, ln_b[0:R, :], ALU.add, [rk, 'lnc'], [rk])

        run_pipeline(NT, [(0, s1), (1, s2), (2, s3)] + list(extra or []))

    pending_epi = [None]

    def ffn(f, kind, tiles, T, ln_idx, final, out_rows, mid_hook=None):
        NT = len(tiles)
        actT_keys = [('actT', ti) for ti in range(NT)]
        for i in range(11):
            ug, kg = w_next(f + 'g%d' % i)
            uu, ku = w_next(f + 'u%d' % i)
            for jj in range(2):
                j = 2 * i + jj
                bg = nb()
                for kc in range(8):
                    mm(pb(bg)[:, 0:T], ug[:, kc, jj * 128:(jj + 1) * 128], actT[:, kc, 0:T], kc == 0, kc == 7,
                       [kg] + actT_keys, [('ps', bg)])
                bu = nb()
                for kc in range(8):
                    mm(pb(bu)[:, 0:T], uu[:, kc, jj * 128:(jj + 1) * 128], actT[:, kc, 0:T], kc == 0, kc == 7,
                       [ku] + actT_keys, [('ps', bu)])
                sq = j % 2
                actf(sg[sq][:, 0:T], pb(bg)[:, 0:T], AF.Silu, [('ps', bg)], [('sg', sq)])
                tt('dve', hT[:, j, 0:T], sg[sq][:, 0:T], pb(bu)[:, 0:T], ALU.mult, [('sg', sq), ('ps', bu)], [('hT', j)])
            w_release(2)
        if mid_hook is not None:
            mid_hook()
        groups = [list(range(NT))]
        for gi, grp in enumerate(groups):
            banks = {ti: (nb(), nb()) for ti in grp}
            for i in range(11):
                ud, kd = w_next(f + 'd%d_%d' % (gi, i))
                for jj in range(2):
                    j = 2 * i + jj
                    for ti in grp:
                        R = tiles[ti]; c0 = 128 * ti
                        for n in range(2):
                            mm(pb(banks[ti][n])[0:R, :], hT[:, j, c0:c0 + R], ud[:, jj, n * 512:(n + 1) * 512],
                               j == 0, j == NJ - 1, [kd, ('hT', j)], [('ps', banks[ti][n])])
                w_release(1)
            def epilogue(extra=None, part=0, grp=grp, banks=banks):
              def pre(ti):
                  R = tiles[ti]; rk = ('res', ti)
                  for n in range(2):
                      stt('dve', res[0:R, ti, n * 512:(n + 1) * 512], res[0:R, ti, n * 512:(n + 1) * 512], 2.0 * ALPHA,
                          pb(banks[ti][n])[0:R, :], ALU.mult, ALU.add, [rk, ('ps', banks[ti][n])], [rk])
              if not final:
                  ln_res_pipeline(tiles, lnc[:, 2 * ln_idx, :], lnc[:, 2 * ln_idx + 1, :], 4.0 * LN_EPS, pre=pre, extra=extra, ln_i=2 * ln_idx)
              else:
                  if part in (0, 1):
                      for ti in grp:
                          pre(ti)
                  if part in (0, 2):
                      for ti in grp:
                          R = tiles[ti]; rk = ('res', ti); yi = ti % 4
                          layernorm(yi, R, 1024, lnc[0:R, 2 * ln_idx, :], lnc[0:R, 2 * ln_idx + 1, :], 4.0 * LN_EPS,
                                    res[0:R, ti, :], [rk], heavy='pool', ybuf_ap=res[:, ti, :], ykey=rk)
                          dma(out_rows(ti, R), res[0:R, ti, :], [rk], [], key=('yout', yi), eng='pool')
            return epilogue

    def run_block(kind, bi):
        if kind == 's':
            tiles = [64]
        else:
            tiles = [128] * 4
        NT = len(tiles); T = sum(tiles)
        actT_keys = [('actT', ti) for ti in range(NT)]

        def x_rows(ti, R):
            if kind == 's':
                return x_s[0:64, :]
            r0 = bi * 512 + ti * 128
            return x_p[r0:r0 + 128, :]

        def y_rows(ti, R):
            if kind == 's':
                return y_s[0:64, :]
            r0 = bi * 512 + ti * 128
            return y_p[r0:r0 + 128, :]

        if kind == 's' or bi == 0:
            cur_extra[0] = S.frontier()
        for ti, R in enumerate(tiles):
            xb_fill(ti, R, dram_src=x_rows(ti, R))
        if pending_epi[0] is not None:
            pending_epi[0](part=1)
        for ti, R in enumerate(tiles):
            xb_transpose(ti, R, 128 * ti)
        if pending_epi[0] is not None:
            pending_epi[0](part=2)
            pending_epi[0] = None
        def load_res():
            for ti, R in enumerate(tiles):
                dma(res[0:R, ti, :], x_rows(ti, R), [], [('res', ti)], key=('res', ti))

        epi1 = ffn('ffn1', kind, tiles, T, 0, False, None, mid_hook=load_res)

        cur_extra[0] = S.frontier()
        uau = [w_next('au0'), w_next('au1')]
        uav = [w_next('av0'), w_next('av1')]
        Wm = WmTs if kind == 's' else WmTp
        bia = biass if kind == 's' else biasp

        def D1(ti):
            R = tiles[ti]; c0 = 128 * ti
            b = nb()
            for c in range(2):
                u, k = uav[c]
                for kc in range(8):
                    mm(pb(b)[0:R, c * 256:(c + 1) * 256], actT[:, kc, c0:c0 + R], u[:, kc, :], kc == 0, kc == 7,
                       [k, ('actT', ti)], [('ps', b)])
            actf(ybuf[0][0:R, 0:512], pb(b)[0:R, :], AF.Gelu_apprx_tanh, [('ps', b)], [('y', 0)])

        def D2(ti):
            R = tiles[ti]; tg = bi * 4 + ti
            layernorm(4, R, 512, alnc[0:R, 0, :], alnc[0:R, 1, :], LN_EPS, vn[0:R, :], ['vn'], ybuf_ap=ybuf[0], ykey=('y', 0))
            if kind == 's':
                dma(cvs[0:64, :], vn[0:64, :], ['vn'], [], key='cvout')
            elif tg == 15:
                dma(cvp[:, :], vn[:, :], ['vn'], [], key='cvout')
            cp('act', vnb[0:R, :], vn[0:R, :], ['vn'], ['vnb'])

        def D3(ti):
            R = tiles[ti]; c0 = 128 * ti
            for g in range(4):
                u, k = uau[g // 2]
                b = nb()
                for kc in range(8):
                    mm(pb(b)[:, 0:R], u[:, kc, (g % 2) * 128:(g % 2) * 128 + 128], actT[:, kc, c0:c0 + R], kc == 0, kc == 7,
                       [k, ('actT', ti)], [('ps', b)])
                actf(guT[:, g, 0:R], pb(b)[:, 0:R], AF.Gelu_apprx_tanh, [('ps', b)], ['guT'])
            b = nb()
            for g in range(4):
                mm(pb(b)[:, g * 128:g * 128 + R], vnb[0:R, g * 128:(g + 1) * 128], Wm[0:R, g, 0:R], True, True,
                   ['vnb', 'WmTp', 'WmTs'], [('ps', b)])
            tt('dve', mixtmp[:, :, 0:R], pb(b).rearrange("p (g t) -> p g t", g=4)[:, :, 0:R], bia[:, :, 0:R], ALU.add,
               [('ps', b), 'biasp', 'biass'], ['mixtmp'])
            tt('pool', a_outT[:, :, c0:c0 + R], mixtmp[:, :, 0:R], guT[:, :, 0:R], ALU.mult, ['mixtmp', 'guT'], [('a_outT', ti)])

        epi1(extra=[(3, D1), (4, D2), (5, D3)])
        w_release(4)

        uq = [w_next('q%d' % c) for c in range(4)]
        ukv = w_next('kv')
        if kind == 's':
            dma(ckb.rearrange("p (b d) -> p b d", b=16), ck.rearrange("b s d -> s b d"), [], ['ckb'], key='ckb', eng='pool')
            for b4 in range(4):
                bk = nb()
                for q in range(8):
                    bb = b4 * 4 + q // 2; kvh = q % 2
                    tr(pbb(bk)[0:64, q * 128:(q + 1) * 128], ckb[:, bb * 128 + kvh * 64:bb * 128 + kvh * 64 + 64], ident,
                       ['ckb', 'ident'], [('ps', bk)])
                for kvh in range(2):
                    src = pbb(bk)[0:64, :].rearrange("p (b k s) -> p b k s", b=4, k=2)[:, :, kvh, :]
                    dst = KT[0:64, kvh, b4 * 512:(b4 + 1) * 512].rearrange("p (b s) -> p b s", b=4)
                    cp('dve', dst, src, [('ps', bk)], ['KT'])
            for kvh in range(2):
                dma(Vaug[:, :, kvh, 0:64], cv.rearrange("b s (k d) -> s b k d", k=2)[:, :, kvh, :], [], ['Vaug'], key='vaug', eng='pool')
            dma(wks[:, 0:124, :], ck[:, 4:128, :], [], [], key='wk1')
            dma(wvs[:, 0:124, :], cv[:, 4:128, :], [], [], key='wv1')
        def make_E(ti):
            R = tiles[ti]
            kcol = 2048 if kind == 's' else (bi * 4 + ti) * 128
            c0 = 128 * ti
            tg = bi * 4 + ti
            if kind == 's':
                cosb = lambda n: ropes[0:R, 0, :].unsqueeze(1).broadcast_to([R, n, 32])
                sinb = lambda n: ropes[0:R, 1, :].unsqueeze(1).broadcast_to([R, n, 32])
            else:
                cosb = lambda n: ropep[0:R, 0, tg, :].unsqueeze(1).broadcast_to([R, n, 32])
                sinb = lambda n: ropep[0:R, 1, tg, :].unsqueeze(1).broadcast_to([R, n, 32])

            def rope(src_bank, nh, dst1, dst2, dkeys):
                xv = pb(src_bank)[0:R, 0:nh * 64].rearrange("p (h c f) -> p h c f", h=nh, c=2)
                x1 = xv[:, :, 0, :]; x2 = xv[:, :, 1, :]
                r = [rt[i][0:R, 0:nh * 32].rearrange("p (h f) -> p h f", h=nh) for i in range(4)]
                rk = ['ropep', 'ropes', ('ps', src_bank)]
                tt('dve', r[0], x1, cosb(nh), ALU.mult, rk, [('rt', 0)])
                tt('dve', r[1], x2, sinb(nh), ALU.mult, rk, [('rt', 1)])
                tt('pool', dst1, r[0], r[1], ALU.subtract, [('rt', 0), ('rt', 1)], dkeys)
                tt('dve', r[2], x2, cosb(nh), ALU.mult, rk, [('rt', 2)])
                tt('dve', r[3], x1, sinb(nh), ALU.mult, rk, [('rt', 3)])
                tt('pool', dst2, r[2], r[3], ALU.add, [('rt', 2), ('rt', 3)], dkeys)

            def E1():
                for c in range(4):
                    u, k = uq[c]
                    b = nb()
                    for kc in range(8):
                        mm(pb(b)[0:R, 0:256], actT[:, kc, c0:c0 + R], u[:, kc, :], kc == 0, kc == 7, [k, ('actT', ti)], [('ps', b)])
                    rope(b, 4, q_rot[0:R, 4 * c:4 * c + 4, 0:32], q_rot[0:R, 4 * c:4 * c + 4, 32:64], ['q_rot'])
                u, k = ukv
                b = nb()
                for kc in range(8):
                    mm(pb(b)[0:R, 0:256], actT[:, kc, c0:c0 + R], u[:, kc, :], kc == 0, kc == 7, [k, ('actT', ti)], [('ps', b)])
                kf3 = kf[0:R, :].rearrange("p (h d) -> p h d", h=2)
                rope(b, 2, kf3[:, :, 0:32], kf3[:, :, 32:64], ['kf'])
                cp('act', vf[0:R, :], pb(b)[0:R, 128:256], [('ps', b)], ['vf'])
                cp('act', kbf[0:R, :], kf[0:R, :], ['kf'], ['kbf'])
                if kind == 's':
                    cp('dve', Vn[0:R, :, 0:64], vf[0:R, :].rearrange("p (k d) -> p k d", k=2), ['vf'], ['Vn'])
                    kcol = 2048
                    for b_ in range(SB):
                        dma(wks[b_, 124:128, :], kf[4 * b_:4 * b_ + 4, :], ['kf'], [], key='wk2')
                        dma(wvs[b_, 124:128, :], vf[4 * b_:4 * b_ + 4, :], ['vf'], [], key='wv2')
                else:
                    cp('dve', Vaug[0:R, tg, :, 0:64], vf[0:R, :].rearrange("p (k d) -> p k d", k=2), ['vf'], [('Vaug', tg)])
                    kcol = tg * 128
                    if tg == 15:
                        dma(wkp[:, :], kf[:, :], ['kf'], [], key='wk2')
                        dma(wvp[:, :], vf[:, :], ['vf'], [], key='wv2')

            def E2():
                b1 = nb(); b2 = nb()
                for h in range(16):
                    bq = b1 if h < 8 else b2
                    tr(pbb(bq)[0:64, (h % 8) * 128:(h % 8) * 128 + R], q_rot[0:R, h, :], ident[0:R, 0:R], ['q_rot', 'ident'], [('ps', bq)])
                cp('act', (qTs if kind == 's' else qT)[0:64, 0:8, 0:R], pbb(b1)[0:64, :].rearrange("p (h t) -> p h t", h=8)[:, :, 0:R], [('ps', b1)], ['qT'])
                cp('act', (qTs if kind == 's' else qT)[0:64, 8:16, 0:R], pbb(b2)[0:64, :].rearrange("p (h t) -> p h t", h=8)[:, :, 0:R], [('ps', b2)], ['qT'])
                b3 = nb()
                for kvh in range(2):
                    tr(pbb(b3)[0:64, kvh * 128:kvh * 128 + R], kbf[0:R, kvh * 64:(kvh + 1) * 64], ident[0:R, 0:R], ['kbf', 'ident'], [('ps', b3)])
                kt_key = 'KT' if kind == 's' else ('KT', tg)
                cp('act', KT[0:64, :, kcol:kcol + R], pbb(b3)[0:64, 0:256].rearrange("p (k t) -> p k t", k=2)[:, :, 0:R], [('ps', b3)], [kt_key])


            def E3():
                if kind == 'p':
                    if tg == 0:
                        blks = [0]; k0 = 0; nk = 128; mk = maskp[:, 128:256]
                    else:
                        blks = [tg - 1, tg]; k0 = (tg - 1) * 128; nk = 256; mk = maskp[:, 0:256]
                    ktk = [('KT', t_) for t_ in blks]
                    vk = [('Vaug', t_) for t_ in blks]
                    nbk = len(blks)
                    stt8 = {}

                    def att_st0(hp):
                        pi = hp % 3
                        b = nb()
                        c2 = slice(2 * hp, 2 * hp + 2)
                        for hh in range(2):
                            h = 2 * hp + hh; kvh = h // 8
                            mm(pb(b)[0:R, hh * 256:hh * 256 + nk], qT[0:64, h, 0:R], KT[0:64, kvh, k0:k0 + nk], True, False,
                               ['qT'] + ktk, [('ps', b)])
                            mm(pb(b)[0:R, hh * 256:hh * 256 + nk], ident[0:R, 0:R], mk[0:R, :], False, True, ['ident', 'maskp'], [('ps', b)])
                        sv = pb(b)[0:R, :].rearrange("p (h k) -> p h k", h=2)[:, :, 0:nk]
                        add('dve', (lambda sv=sv: lambda e: e.tensor_reduce(out=sm['mx'][0:R, c2], in_=sv, axis=AX.X, op=ALU.max))(), [('ps', b)], [('mx', hp)])
                        stt('dve', sm['negm'][0:R, c2], sm['mx'][0:R, c2], -SCALE, negsinks[0:R, c2], ALU.mult, ALU.min,
                            [('mx', hp), 'negsinks'], [('negm', hp)])
                        tt('dve', sm['tmp2'][0:R, c2], sinks[0:R, c2], sm['negm'][0:R, c2], ALU.add, [('negm', hp), 'sinks'], [('tmp2', hp)])
                        actf(sm['es'][0:R, c2], sm['tmp2'][0:R, c2], AF.Exp, [('tmp2', hp)], [('es', hp)])
                        P3 = Pb[pi][0:R, :].rearrange("p (h k) -> p h k", h=2)
                        for hh in range(2):
                            actf(P3[:, hh, 0:nk], pb(b)[0:R, hh * 256:hh * 256 + nk], AF.Exp, [('ps', b), ('negm', hp)], [('P', pi)],
                                 bias=sm['negm'][0:R, 2 * hp + hh:2 * hp + hh + 1], scale=SCALE)

                    def att_st1(hp):
                        pi = hp % 3
                        p3i = hp % 3
                        P3 = Pb[p3i][0:R, :].rearrange("p (h k) -> p h k", h=2)
                        bt = nb()
                        for hh in range(2):
                            for bj in range(nbk):
                                sl = hh * 2 + bj
                                tr(pbb(bt)[:, sl * 128:sl * 128 + R], P3[:, hh, bj * 128:(bj + 1) * 128], ident[0:R, 0:R], [('P', p3i), 'ident'], [('ps', bt)])
                        PT3 = PT[pi].rearrange("p (s t) -> p s t", s=4)
                        ceng = 'act'
                        if nbk == 2:
                            cp(ceng, PT3[:, :, 0:R], pbb(bt)[:, 0:512].rearrange("p (s t) -> p s t", s=4)[:, :, 0:R], [('ps', bt)], [('PT', pi)])
                        else:
                            src = pbb(bt)[:, 0:512].rearrange("p (h s t) -> p h s t", h=2, s=2)[:, :, 0, 0:R]
                            dst = PT[pi].rearrange("p (h s t) -> p h s t", h=2, s=2)[:, :, 0, 0:R]
                            cp(ceng, dst, src, [('ps', bt)], [('PT', pi)])

                    def att_st2(hp):
                        pi = hp % 3
                        c2 = slice(2 * hp, 2 * hp + 2)
                        PT3 = PT[pi].rearrange("p (s t) -> p s t", s=4)
                        bo = nb()
                        for hh in range(2):
                            h = 2 * hp + hh; kvh = h // 8
                            for bj in range(nbk):
                                sl = hh * 2 + bj
                                mm(pb(bo)[0:R, hh * 128:hh * 128 + 65], PT3[:, sl, 0:R], Vaug[:, blks[bj], kvh, 0:65], bj == 0, bj == nbk - 1,
                                   [('PT', pi), 'Vaug'] + vk, [('ps', bo)])
                        O3 = pb(bo)[0:R, 0:256].rearrange("p (h d) -> p h d", h=2)
                        tt('dve', sm['den'][0:R, c2], O3[:, :, 64], sm['es'][0:R, c2], ALU.add, [('ps', bo), ('es', hp)], [('den', hp)])
                        add('dve', lambda e: e.reciprocal(out=sm['rden'][0:R, c2], in_=sm['den'][0:R, c2]), [('den', hp)], [('rden', hp)])
                        tt('dve', b_out[0:R, 2 * hp:2 * hp + 2, :], O3[:, :, 0:64],
                           sm['rden'][0:R, c2].unsqueeze(2).broadcast_to([R, 2, 64]), ALU.mult, [('ps', bo), ('rden', hp)], ['b_out'])

                    for step in range(8 + 4):
                        if step < 8:
                            att_st0(step)
                        if 0 <= step - 2 < 8:
                            att_st1(step - 2)
                        if 0 <= step - 4 < 8:
                            att_st2(step - 4)
                else:
                    widths = [512, 512, 512, 512, 64]
                    sst = {}

                    def sa0(hp):
                        kvh = hp // 4
                        lhs2 = qT_raw[0:64, 128 * hp:128 * hp + 128]
                        sbk = []
                        for c in range(5):
                            w = widths[c]
                            b = nb(); sbk.append(b)
                            mm(pb(b)[:, 0:w], lhs2, KT[0:64, kvh, c * 512:c * 512 + w], True, False, ['qT', 'KT'], [('ps', b)])
                            mm(pb(b)[:, 0:w], ident, masks[:, c * 512:c * 512 + w], False, True, ['ident', 'masks'], [('ps', b)])
                            add('dve', (lambda b=b, w=w, c=c, hp=hp: lambda e: e.tensor_reduce(out=mx5s[hp % 2][:, c:c + 1], in_=pb(b)[:, 0:w], axis=AX.X, op=ALU.max))(),
                                [('ps', b)], [('mx5', hp % 2)])
                        c1 = slice(hp, hp + 1)
                        add('dve', (lambda hp=hp, c1=c1: lambda e: e.tensor_reduce(out=sm['mx'][:, c1], in_=mx5s[hp % 2][:, 0:5], axis=AX.X, op=ALU.max))(),
                            [('mx5', hp % 2)], [('mx', hp)])
                        stt('dve', sm['negm'][:, c1], sm['mx'][:, c1], -SCALE, negsinks2[:, c1], ALU.mult, ALU.min, [('mx', hp), 'negsinks2'], [('negm', hp)])
                        tt('dve', sm['tmp2'][:, c1], sinks2[:, c1], sm['negm'][:, c1], ALU.add, [('negm', hp), 'sinks2'], [('tmp2', hp)])
                        actf(sm['es'][:, c1], sm['tmp2'][:, c1], AF.Exp, [('tmp2', hp)], [('es', hp)])
                        sst[hp] = sbk

                    def sa0b(hp):
                        c1 = slice(hp, hp + 1)
                        sbk = sst[hp]
                        for c in range(5):
                            w = widths[c]
                            actf(Ps[:, c * 512:c * 512 + w], pb(sbk[c])[:, 0:w], AF.Exp, [('ps', sbk[c]), ('negm', hp)], ['Ps'],
                                 bias=sm['negm'][:, c1], scale=SCALE)

                    def sa1(hp):
                        bts = [nb(), nb(), nb()]
                        for bj in range(17):
                            kw = 128 if bj < 16 else 64
                            bt = bts[bj // 8]
                            sl = bj % 8
                            tr(pbb(bt)[0:kw, sl * 128:(sl + 1) * 128], Ps[:, bj * 128:bj * 128 + kw], ident, ['Ps', 'ident'], [('ps', bt)])
                        cp('dve', PTs[:, 0:8, :], pbb(bts[0]).rearrange("p (s t) -> p s t", s=8), [('ps', bts[0])], ['PTs'])
                        cp('act', PTs[:, 8:16, :], pbb(bts[1]).rearrange("p (s t) -> p s t", s=8), [('ps', bts[1])], ['PTs'])
                        cp('dve', PTs[0:64, 16, :], pbb(bts[2])[0:64, 0:128], [('ps', bts[2])], ['PTs'])

                    def sa2(hp):
                        kvh = hp // 4
                        c1 = slice(hp, hp + 1)
                        bo = nb()
                        for bj in range(17):
                            if bj < 16:
                                mm(pb(bo)[:, 0:65], PTs[:, bj, :], Vaug[:, bj, kvh, 0:65], bj == 0, False, ['PTs', 'Vaug'], [('ps', bo)])
                            else:
                                mm(pb(bo)[:, 0:65], PTs[0:64, 16, :], Vn[0:64, kvh, 0:65], False, True, ['PTs', 'Vn'], [('ps', bo)])
                        tt('dve', sm['den'][:, c1], pb(bo)[:, 64:65], sm['es'][:, c1], ALU.add, [('ps', bo), ('es', hp)], [('den', hp)])
                        add('dve', (lambda c1=c1: lambda e: e.reciprocal(out=sm['rden'][:, c1], in_=sm['den'][:, c1]))(), [('den', hp)], [('rden', hp)])
                        tsc('dve', tmpO, pb(bo)[:, 0:64], sm['rden'][:, c1], None, ALU.mult, None, [('ps', bo), ('rden', hp)], ['tmpO'])
                        cp('dve', b_out[0:64, 2 * hp, :], tmpO[0:64, :], ['tmpO'], ['b_out'])
                        cp('dve', b_out[0:64, 2 * hp + 1, :], tmpO[64:128, :], ['tmpO'], ['b_out'])

                    sa0(0)
                    sa0b(0)
                    for hp in range(8):
                        if hp + 1 < 8:
                            sa0(hp + 1)
                        sa1(hp)
                        if hp + 1 < 8:
                            sa0b(hp + 1)
                        sa2(hp)
            def E4():
                b = nb()
                bo2 = b_out[0:R].rearrange("p h d -> p (h d)")
                for kc in range(8):
                    tr(pbb(b)[:, kc * 128:kc * 128 + R], bo2[:, kc * 128:(kc + 1) * 128], ident[0:R, 0:R], ['b_out', 'ident'], [('ps', b)])
                cp('act', b_outT[:, :, c0:c0 + R], pbb(b).rearrange("p (k t) -> p k t", k=8)[:, :, 0:R], [('ps', b)], [('b_outT', ti)])

            return (E1, E2, E3, E4)

        Es = [make_E(ti) for ti in range(NT)]
        run_pipeline(NT, [(0, lambda t: Es[t][0]()), (1, lambda t: Es[t][1]()), (2, lambda t: Es[t][2]()), (3, lambda t: Es[t][3]())])
        w_release(5)

        cur_extra[0] = S.frontier()
        for n in range(2):
            upa = w_next('pa%d' % n); uga = [w_next('ga%d' % (2 * n)), w_next('ga%d' % (2 * n + 1))]
            for ti, R in enumerate(tiles):
                c0 = 128 * ti
                ba = nb()
                for g in range(4):
                    mm(pb(ba)[0:R, :], a_outT[:, g, c0:c0 + R], upa[0][:, g, :], g == 0, g == 3, [upa[1], ('a_outT', ti)], [('ps', ba)])
                bg = nb()
                for c in range(2):
                    for kc in range(8):
                        mm(pb(bg)[0:R, c * 256:(c + 1) * 256], actT[:, kc, c0:c0 + R], uga[c][0][:, kc, :], kc == 0, kc == 7,
                           [uga[c][1], ('actT', ti)], [('ps', bg)])
                si = ti % 2
                actf(sig[si][0:R, :], pb(bg)[0:R, :], AF.Sigmoid, [('ps', bg)], [('sig', si)])
                tt('dve', t1[0:R, ti, :], sig[si][0:R, :], pb(ba)[0:R, :], ALU.mult, [('sig', si), ('ps', ba)], [('t1', ti)])
            w_release(3)
            upb = [w_next('pb0_%d' % n), w_next('pb1_%d' % n)]
            ugb = [w_next('gb%d' % (2 * n)), w_next('gb%d' % (2 * n + 1))]
            for ti, R in enumerate(tiles):
                c0 = 128 * ti
                bb_ = nb()
                for kc in range(8):
                    mm(pb(bb_)[0:R, :], b_outT[:, kc, c0:c0 + R], upb[kc // 4][0][:, kc % 4, :], kc == 0, kc == 7,
                       [upb[kc // 4][1], ('b_outT', ti)], [('ps', bb_)])
                bg = nb()
                for c in range(2):
                    for kc in range(8):
                        mm(pb(bg)[0:R, c * 256:(c + 1) * 256], actT[:, kc, c0:c0 + R], ugb[c][0][:, kc, :], kc == 0, kc == 7,
                           [ugb[c][1], ('actT', ti)], [('ps', bg)])
                si = ti % 2
                actf(sig[si][0:R, :], pb(bg)[0:R, :], AF.Sigmoid, [('ps', bg)], [('sig', si)])
                tt('dve', t2[0:R, :], sig[si][0:R, :], pb(bb_)[0:R, :], ALU.mult, [('sig', si), ('ps', bb_)], ['t2'])
                tt('pool', mergedb[0:R, ti, n * 512:(n + 1) * 512], t2[0:R, :], t1[0:R, ti, :], ALU.add, ['t2', ('t1', ti)], [('mergedb', ti)])
            w_release(4)
        uwo = [[w_next('wo0_0'), w_next('wo1_0')], [w_next('wo0_1'), w_next('wo1_1')]]

        def M1(ti):
            R = tiles[ti]; c0 = 128 * ti
            b = nb()
            for kc in range(8):
                tr(pbb(b)[:, kc * 128:kc * 128 + R], mergedb[0:R, ti, kc * 128:(kc + 1) * 128], ident[0:R, 0:R], [('mergedb', ti), 'ident'], [('ps', b)])
            cp('act', actT[:, :, c0:c0 + R], pbb(b).rearrange("p (k t) -> p k t", k=8)[:, :, 0:R], [('ps', b)], [('actT', ti)])

        def M2(ti):
            R = tiles[ti]; c0 = 128 * ti
            rk = ('res', ti)
            bks = []
            for n in range(2):
                b = nb(); bks.append(b)
                for kc in range(8):
                    u, k = uwo[n][kc // 4]
                    mm(pb(b)[0:R, :], actT[:, kc, c0:c0 + R], u[:, kc % 4, :], kc == 0, kc == 7, [k, ('actT', ti)], [('ps', b)])
            for n in range(2):
                stt('dve', res[0:R, ti, n * 512:(n + 1) * 512], res[0:R, ti, n * 512:(n + 1) * 512], ALPHA,
                    pb(bks[n])[0:R, :], ALU.mult, ALU.add, [rk, ('ps', bks[n])], [rk])

        ln_res_pipeline(tiles, lnc[:, 2, :], lnc[:, 3, :], LN_EPS, extra=[(-2, M1), (-1, M2)], ln_i=2)
        w_release(4)
        cur_extra[0] = S.frontier()
        pending_epi[0] = ffn('ffn2', kind, tiles, T, 2, True, y_rows)

    for kind, b in blocks:
        run_block(kind, b)
    pending_epi[0]()
    assert WS.cur == len(all_units), (WS.cur, len(all_units))
    S.emit(nc)
    st.close()
    return nc


_PROG = {}


def _consts():
    c = {}
    c['ident'] = np.eye(128, dtype=np.float32)
    s = np.arange(128)
    c['triu'] = (s[:, None] <= s[None, :]).astype(np.float32)
    bs = np.arange(64)
    c['bdm'] = (((bs[:, None] // 4) == (bs[None, :] // 4)) & ((bs[:, None] % 4) <= (bs[None, :] % 4))).astype(np.float32)
    inv = 10000.0 ** (-(np.arange(32, dtype=np.float64) / 32.0))
    p = np.arange(128)
    t = np.arange(16)
    pos = (t[None, :] * 128 + p[:, None]).astype(np.float64)
    ang = pos[:, :, None] * inv[None, None, :]
    rp = np.stack([np.cos(ang), np.sin(ang)], axis=1)
    c['rope_p'] = np.ascontiguousarray(rp.reshape(128, 1024)).astype(np.float32)
    pos_s = (PAST_LEN + (np.arange(64) % 4)).astype(np.float64)
    ang_s = pos_s[:, None] * inv[None, :]
    rs = np.stack([np.cos(ang_s), np.sin(ang_s)], axis=1)
    c['rope_s'] = np.ascontiguousarray(rs.reshape(64, 64)).astype(np.float32)
    tq = np.arange(128)[:, None]; sk = np.arange(128)[None, :]
    mprev = np.where(sk > tq, 0.0, NEG); mcur = np.where(sk <= tq, 0.0, NEG)
    c['mask_p'] = np.concatenate([mprev, mcur], axis=1).astype(np.float32)
    q = np.arange(64); qb = q // 4; qt = q % 4
    col = np.arange(2048); cb = col // 128; cs = col % 128
    m1 = np.where((cb[None, :] == qb[:, None]) & (cs[None, :] >= qt[:, None] + 1), 0.0, NEG)
    coln = np.arange(64); nb_ = coln // 4; nt = coln % 4
    m2 = np.where((nb_[None, :] == qb[:, None]) & (nt[None, :] <= qt[:, None]), 0.0, NEG)
    ms = np.concatenate([m1, m2], axis=1).astype(np.float32)
    c['mask_s'] = np.ascontiguousarray(np.concatenate([ms, ms], axis=0))
    return c


def kernel(x_prompt, x_sample, cache_win_k, cache_win_v, ffn1_up, ffn1_down, ln1_g, ln1_b,
           w_in, a_ln_g, a_ln_b, a_ws, a_bs, attn_sinks, w_pa, w_pb, w_o, ln2_g, ln2_b,
           ffn2_up, ffn2_down, ln3_g, ln3_b):
    f = lambda a: np.ascontiguousarray(np.asarray(a, dtype=np.float32))
    if 'nc' not in _PROG:
        _PROG['nc'] = build_program()
    nc = _PROG['nc']
    shared = dict(
        ffn1_up=f(ffn1_up[0]), ffn1_down=f(ffn1_down[0]), w_in=f(w_in[0]), w_pa=f(w_pa[0]), w_pb=f(w_pb[0]),
        w_o=f(w_o[0]), ffn2_up=f(ffn2_up[0]), ffn2_down=f(ffn2_down[0]))
    lnrow = np.concatenate([np.asarray(v[0]) for v in (ln1_g, ln1_b, ln2_g, ln2_b, ln3_g, ln3_b)])
    shared['lnc'] = f(np.broadcast_to(lnrow[None, :], (128, 6144)))
    shared['lncol'] = f(np.transpose(lnrow.reshape(6, 8, 128), (2, 0, 1)).reshape(128, 48))
    arow = np.concatenate([np.asarray(a_ln_g[0]), np.asarray(a_ln_b[0])])
    shared['alnc'] = f(np.broadcast_to(arow[None, :], (128, 1024)))
    ws = np.asarray(a_ws[0])
    shared['wsT_p'] = f(np.transpose(ws, (2, 0, 1)).reshape(128, 512))
    bsv = np.asarray(a_bs[0])
    shared['bias_p'] = f(np.broadcast_to(bsv[None], (128, 4, 128)).reshape(128, 512))
    subT = np.transpose(ws[:, 0:4, 0:4], (2, 0, 1))
    shared['wsT_s'] = f(np.tile(subT[None, :, :, None, :], (16, 1, 1, 16, 1)).reshape(64, 256))
    shared['bias_s'] = f(np.broadcast_to(np.tile(bsv[:, 0:4], (1, 16))[None], (128, 4, 64)).reshape(128, 256))
    shared['sinks'] = f(np.broadcast_to(np.asarray(attn_sinks[0])[None, :], (128, 16)))
    shared['sinks2'] = f(np.repeat(np.asarray(attn_sinks[0]).reshape(8, 2).T, 64, axis=0))
    shared.update(_consts())
    xp = np.asarray(x_prompt); xs = np.asarray(x_sample)
    ckf = np.asarray(cache_win_k); cvf = np.asarray(cache_win_v)
    in_maps = []
    for c in range(NCORES):
        m = dict(shared)
        m['x_p'] = f(xp[c])
        m['x_s'] = f(xs[SB * c:SB * (c + 1)].reshape(64, 1024))
        m['ck'] = f(ckf[0, SB * c:SB * (c + 1)].reshape(SB, 128, 128))
        m['cv'] = f(cvf[0, SB * c:SB * (c + 1)].reshape(SB, 128, 128))
        in_maps.append(m)
    res = run_bass_kernel_spmd(nc, in_maps, core_ids=list(range(NCORES)))
    R = res.results
    y_prompt = np.stack([R[c]['y_p'] for c in range(NCORES)]).astype(np.float32)
    y_sample = np.concatenate([R[c]['y_s'].reshape(SB, 4, 1024) for c in range(NCORES)]).astype(np.float32)
    wkp = np.stack([R[c]['wkp'].reshape(128, 2, 64) for c in range(NCORES)])[None].astype(np.float32)
    wvp = np.stack([R[c]['wvp'].reshape(128, 2, 64) for c in range(NCORES)])[None].astype(np.float32)
    wks = np.concatenate([R[c]['wks'].reshape(SB, 128, 2, 64) for c in range(NCORES)])[None].astype(np.float32)
    wvs = np.concatenate([R[c]['wvs'].reshape(SB, 128, 2, 64) for c in range(NCORES)])[None].astype(np.float32)
    cvp = np.stack([R[c]['cvp'] for c in range(NCORES)])[None].astype(np.float32)
    cvs = np.concatenate([R[c]['cvs'].reshape(SB, 4, 512) for c in range(NCORES)])[None].astype(np.float32)
    return (y_prompt, y_sample, wkp, wvp, wks, wvs, cvp, cvs)
```

```python
import contextlib
from concourse.bass_utils import run_bass_kernel_spmd
import numpy as np
import concourse.bass as bass
import concourse.mybir as mybir

F32 = mybir.dt.float32
BF16 = mybir.dt.bfloat16
AF = mybir.ActivationFunctionType
ALU = mybir.AluOpType
AX = mybir.AxisListType

ENGS = ['pe', 'act', 'dve', 'pool', 'sp']


class Sched:
    def __init__(self):
        self.ops = {e: [] for e in ENGS}
        self.w = {}
        self.r = {}
        self.dma_count = {}
        self.extra = {}

    def frontier(self):
        f = {}
        for e in ENGS:
            if e == 'sp':
                continue
            if self.ops[e]:
                for i in range(len(self.ops[e]) - 1, -1, -1):
                    if self.ops[e][i]['tok'][0] == 'c':
                        f[('c', e)] = self.ops[e][i]['tok']
                        break
        for k, n in self.dma_count.items():
            f[('d', k)] = ('d', k, n)
        return f

    def add(self, eng, fn, reads=(), writes=(), dma=None, extra=None):
        deps = set()
        for k in reads:
            deps |= set(self.w.get(k, {}).values())
        for k in writes:
            deps |= set(self.w.get(k, {}).values())
            deps |= set(self.r.get(k, {}).values())
        if extra:
            deps |= set(extra.values())
        idx = len(self.ops[eng])
        if dma is not None:
            seq = self.dma_count.get(dma, 0) + 1
            self.dma_count[dma] = seq
            tok = ('d', dma, seq)
            src = ('d', dma)
        else:
            tok = ('c', eng, idx)
            src = ('c', eng)
        if eng == 'pe':
            deps = {d for d in deps if not (d[0] == 'c' and d[1] == 'pe')}
        deps.discard(tok)
        op = dict(eng=eng, fn=fn, deps=deps, tok=tok, needed=False)
        self.ops[eng].append(op)
        for d in deps:
            if d[0] == 'c':
                self.ops[d[1]][d[2]]['needed'] = True
        for k in reads:
            self.r.setdefault(k, {})[src] = tok
        for k in writes:
            self.w.setdefault(k, {})[src] = tok
        return tok

    def emit(self, nc, final_wait_eng='sp'):
        import contextlib
        semval = {}
        for e in ENGS:
            c = 0
            for i, op in enumerate(self.ops[e]):
                if op['tok'][0] == 'c' and op['needed']:
                    c += 1
                    semval[op['tok']] = c
        dma_keys = list(self.dma_count.keys())
        with nc.cleanup_on_exit():
          with contextlib.ExitStack() as st:
            csem = {e: nc.alloc_semaphore(name='c_' + e) for e in ENGS}
            dsem = {k: nc.alloc_semaphore(name='d%d' % i) for i, k in enumerate(dma_keys)}
            block = st.enter_context(nc.Block())

            def run(eng_name, eobj):
                waited = {}
                for op in self.ops[eng_name]:
                    need = {}
                    for d in op['deps']:
                        if d[0] == 'c':
                            sem = csem[d[1]]
                            val = semval[d]
                            kk = ('c', d[1])
                        else:
                            sem = dsem[d[1]]
                            val = 16 * d[2]
                            kk = ('d', d[1])
                        if need.get(kk, (None, 0))[1] < val:
                            need[kk] = (sem, val)
                    for kk, (sem, val) in need.items():
                        if waited.get(kk, 0) < val:
                            eobj.wait_ge(sem, val)
                            waited[kk] = val
                    inst = op['fn'](eobj)
                    if op['tok'][0] == 'd':
                        inst.then_inc(dsem[op['tok'][1]], 16)
                    elif op['needed']:
                        inst.then_inc(csem[eng_name], 1)
                if eng_name == final_wait_eng:
                    for k, n in self.dma_count.items():
                        if waited.get(('d', k), 0) < 16 * n:
                            eobj.wait_ge(dsem[k], 16 * n)

            @block.tensor
            def _(e):
                run('pe', e)

            @block.scalar
            def _(e):
                run('act', e)

            @block.vector
            def _(e):
                run('dve', e)

            @block.gpsimd
            def _(e):
                run('pool', e)

            @block.sync
            def _(e):
                run('sp', e)
          nc.all_engine_barrier()


class Arena:
    def __init__(self, t, nbytes):
        self.t = t
        self.nbytes = nbytes
        self.off = 0
        self.peak = 0

    def alloc(self, nfree, dtype, parts=128):
        esz = 2 if dtype == BF16 else 4
        nb = (nfree * esz + 31) // 32 * 32
        assert self.off + nb <= self.nbytes, ("arena overflow", self.off, nb, self.nbytes)
        a = self.t[0:parts, self.off // 4:(self.off + nb) // 4]
        if dtype != F32:
            a = a.bitcast(dtype)
        a = a[:, 0:nfree]
        self.off += nb
        self.peak = max(self.peak, self.off)
        return a

    def mark(self):
        return self.off

    def reset(self, m):
        self.off = m


D_MODEL = 1024; SEQ = 2048; DEC_SEQ = 4; PAST_LEN = 16384
N_HEADS = 16; N_KV = 2; HD = 64; D_FF = 2816; NJ = 22
LN_EPS = 1e-5
ALPHA = 2.0 ** 0.25
SCALE = HD ** -0.5
NEG = -30000.0
NCORES = 8
SB = 16


def build_program(do_prompt=True, n_pblocks=4):
    nc = bass.Bass("TRN2", target_bir_lowering=False)

    def din(name, shape):
        return nc.dram_tensor(name, list(shape), F32, kind="ExternalInput").ap()

    def dout(name, shape):
        return nc.dram_tensor(name, list(shape), F32, kind="ExternalOutput").ap()

    x_p = din("x_p", [SEQ, 1024]); x_s = din("x_s", [64, 1024])
    ck = din("ck", [SB, 128, 128]); cv = din("cv", [SB, 128, 128])
    Wd = dict(
        ffn1_up=din("ffn1_up", [1024, 5632]), ffn1_down=din("ffn1_down", [2816, 1024]),
        w_in=din("w_in", [1024, 4352]), w_pa=din("w_pa", [512, 1024]), w_pb=din("w_pb", [1024, 1024]),
        w_o=din("w_o", [1024, 1024]), ffn2_up=din("ffn2_up", [1024, 5632]), ffn2_down=din("ffn2_down", [2816, 1024]))
    d_lnc = din("lnc", [128, 6 * 1024]); d_alnc = din("alnc", [128, 1024]); d_lncol = din("lncol", [128, 48])
    d_wsTp = din("wsT_p", [128, 512]); d_biasp = din("bias_p", [128, 512])
    d_wsTs = din("wsT_s", [64, 256]); d_biass = din("bias_s", [128, 256])
    d_sinks = din("sinks", [128, 16]); d_sinks2 = din("sinks2", [128, 8])
    d_ident = din("ident", [128, 128]); d_triu = din("triu", [128, 128]); d_bdm = din("bdm", [64, 64])
    d_ropep = din("rope_p", [128, 1024]); d_ropes = din("rope_s", [64, 64])
    d_maskp = din("mask_p", [128, 256]); d_masks = din("mask_s", [128, 2112])

    y_p = dout("y_p", [SEQ, 1024]); y_s = dout("y_s", [64, 1024])
    wkp = dout("wkp", [128, 128]); wvp = dout("wvp", [128, 128])
    wks = dout("wks", [SB, 128, 128]); wvs = dout("wvs", [SB, 128, 128])
    cvp = dout("cvp", [128, 512]); cvs = dout("cvs", [64, 512])

    S = Sched()
    st = contextlib.ExitStack()
    ARENA_B = 206 * 1024
    art = st.enter_context(nc.sbuf_tensor("arena", [128, ARENA_B // 4], F32))
    ps = st.enter_context(nc.psum_tensor("ps", [128, 4096], F32))
    A = Arena(art, ARENA_B)

    def pb(b):
        return ps[:, 512 * b:512 * (b + 1)]

    def pbb(b):
        return ps[:, 512 * b:512 * (b + 1)].bitcast(BF16)

    bank_i = [0]

    def nb():
        b = bank_i[0]
        bank_i[0] = (b + 1) % 8
        return b

    cur_extra = [None]
    alias_keys = set()

    def add(eng, fn, reads=(), writes=(), dma=None):
        ex = None
        if cur_extra[0] is not None:
            ex = cur_extra[0]
        return S.add(eng, fn, reads=list(reads), writes=list(writes), dma=dma, extra=ex)

    def mm(out, lhsT, rhs, start, stop, reads, writes):
        add('pe', lambda e: e.matmul(out, lhsT=lhsT, rhs=rhs, start=start, stop=stop), reads, writes)

    def tr(out, in_, idn, reads, writes):
        add('pe', lambda e: e.transpose(out, in_, idn), reads, writes)

    def actf(out, in_, func, reads, writes, bias=None, scale=None, eng='act'):
        kw = {}
        if bias is not None:
            kw['bias'] = bias
        if scale is not None:
            kw['scale'] = scale
        add(eng, lambda e: e.activation(out=out, in_=in_, func=func, **kw), reads, writes)

    def tt(eng, out, in0, in1, op, reads, writes):
        add(eng, lambda e: e.tensor_tensor(out=out, in0=in0, in1=in1, op=op), reads, writes)

    def tsc(eng, out, in0, s1, s2, op0, op1, reads, writes):
        if op1 is None:
            add(eng, lambda e: e.tensor_scalar(out=out, in0=in0, scalar1=s1, scalar2=None, op0=op0), reads, writes)
        else:
            add(eng, lambda e: e.tensor_scalar(out=out, in0=in0, scalar1=s1, scalar2=s2, op0=op0, op1=op1), reads, writes)

    def stt(eng, out, in0, scalar, in1, op0, op1, reads, writes):
        add(eng, lambda e: e.scalar_tensor_tensor(out=out, in0=in0, scalar=scalar, in1=in1, op0=op0, op1=op1), reads, writes)

    def cp(eng, out, in_, reads, writes):
        if eng == 'act':
            add(eng, lambda e: e.activation(out=out, in_=in_, func=AF.Copy), reads, writes)
        else:
            add(eng, lambda e: e.tensor_copy(out=out, in_=in_), reads, writes)

    def dma(out, in_, reads, writes, key, eng='sp'):
        add(eng, lambda e: e.dma_start(out=out, in_=in_), reads, writes, dma=key)

    ident = A.alloc(128, BF16)
    ident32 = A.alloc(128, F32)
    lncol = A.alloc(48, F32).rearrange("p (a k) -> p a k", a=6)
    lnc = A.alloc(6 * 1024, F32).rearrange("p (a n) -> p a n", a=6)
    alnc = A.alloc(1024, F32).rearrange("p (a n) -> p a n", a=2)
    WmTp = A.alloc(512, BF16).rearrange("p (g t) -> p g t", g=4)
    biasp = A.alloc(512, F32).rearrange("p (g t) -> p g t", g=4)
    WmTs = A.alloc(256, BF16).rearrange("p (g t) -> p g t", g=4)
    biass = A.alloc(256, F32).rearrange("p (g t) -> p g t", g=4)
    sinks = A.alloc(16, F32); negsinks = A.alloc(16, F32)
    sinks2 = A.alloc(8, F32); negsinks2 = A.alloc(8, F32)
    ropep = A.alloc(1024, F32).rearrange("p (c t f) -> p c t f", c=2, t=16)
    ropes = A.alloc(64, F32).rearrange("p (c f) -> p c f", c=2)
    maskp = A.alloc(256, BF16); masks = A.alloc(2112, BF16)
    mhalf = A.alloc(1, F32)
    stage = [A.alloc(1024, F32) for _ in range(4)]
    NR = 8
    ring = [A.alloc(2048, BF16) for _ in range(NR)]
    res = A.alloc(4 * 1024, F32).rearrange("p (t n) -> p t n", t=4)
    actT = A.alloc(8 * 512, BF16).rearrange("p (k t) -> p k t", k=8)
    KT = A.alloc(2 * 2112, BF16).rearrange("p (k t) -> p k t", k=2)
    Vaug = A.alloc(16 * 2 * 66, BF16).rearrange("p (b k d) -> p b k d", b=16, k=2)
    Vn = A.alloc(2 * 66, BF16).rearrange("p (k d) -> p k d", k=2)
    ybuf = [A.alloc(1024, F32) for _ in range(1)]
    xb = [A.alloc(1024, BF16) for _ in range(4)]
    lnst = [A.alloc(16, F32) for _ in range(5)]
    lnst2 = [A.alloc(8, F32) for _ in range(4)]
    m_alias = A.mark()
    tmpc = A.alloc(5632, F32)
    A.reset(m_alias)
    hT = A.alloc(NJ * 512, BF16).rearrange("p (j t) -> p j t", j=NJ)
    sg = [A.alloc(512, F32) for _ in range(2)]
    A.reset(m_alias)
    guT = A.alloc(4 * 128, F32).rearrange("p (g t) -> p g t", g=4)
    a_outT = A.alloc(4 * 512, BF16).rearrange("p (g t) -> p g t", g=4)
    gv = A.alloc(512, F32); vn = A.alloc(512, F32); vnb = A.alloc(512, BF16)
    mixtmp = A.alloc(512, F32).rearrange("p (g t) -> p g t", g=4)
    b_outT = A.alloc(8 * 512, BF16).rearrange("p (k t) -> p k t", k=8)
    sig = [A.alloc(512, F32) for _ in range(2)]
    t1 = A.alloc(4 * 512, F32).rearrange("p (t n) -> p t n", t=4)
    t2 = A.alloc(512, F32)
    m_e = A.mark()
    q_rot = A.alloc(1024, BF16).rearrange("p (h d) -> p h d", h=16)
    qT_raw = A.alloc(16 * 128, BF16)
    qT = qT_raw.rearrange("p (h t) -> p h t", h=16)
    qTs = qT_raw[:, 0:1024].rearrange("p (h t) -> p h t", h=16)
    kf = A.alloc(128, F32); kbf = A.alloc(128, BF16); vf = A.alloc(128, F32)
    rt = [A.alloc(128, F32) for _ in range(4)]
    Pb = [A.alloc(512, BF16) for _ in range(3)]
    PT = [A.alloc(512, BF16) for _ in range(3)]
    Ps = A.alloc(2112, BF16)
    PTs = A.alloc(17 * 128, BF16).rearrange("p (b t) -> p b t", b=17)
    tmpO = A.alloc(64, BF16)
    b_out = A.alloc(1024, BF16).rearrange("p (h d) -> p h d", h=16)
    ckb = A.alloc(2048, BF16)
    mx5 = A.alloc(8, F32)
    mx5s = [mx5, A.alloc(8, F32)]
    sm = {n: A.alloc(16, F32) for n in ['mx', 'negm', 'es', 'den', 'rden', 'tmp2']}
    A.reset(m_e)
    mergedb = A.alloc(4 * 1024, BF16).rearrange("p (t n) -> p t n", t=4)
    print("arena peak bytes", A.peak)

    def load_const(dst, src, key, parts=128):
        dma(dst, src, [], [key], key=('c', key), eng='act')

    load_const(lnc.rearrange("p a n -> p (a n)"), d_lnc, 'lnc')
    load_const(alnc.rearrange("p a n -> p (a n)"), d_alnc, 'alnc')
    load_const(lncol.rearrange("p a k -> p (a k)"), d_lncol, 'lncol')
    load_const(ident32, d_ident, 'ident32')
    load_const(biasp.rearrange("p g t -> p (g t)"), d_biasp, 'biasp')
    load_const(biass.rearrange("p g t -> p (g t)"), d_biass, 'biass')
    load_const(sinks, d_sinks, 'sinks')
    load_const(sinks2, d_sinks2, 'sinks2')
    load_const(ropep.rearrange("p c t f -> p (c t f)"), d_ropep, 'ropep')
    load_const(ropes[0:64].rearrange("p c f -> p (c f)"), d_ropes, 'ropes')
    add('pool', lambda e: e.memset(mhalf, -0.5), [], ['mhalf'])
    tsc('dve', negsinks, sinks, -1.0, None, ALU.mult, None, ['sinks'], ['negsinks'])
    tsc('dve', negsinks2, sinks2, -1.0, None, ALU.mult, None, ['sinks2'], ['negsinks2'])
    t_id = tmpc[:, 0:128]; t_ws = tmpc[:, 128:640]; t_tri = tmpc[:, 640:768]
    t_wss = tmpc[0:64, 768:1024]; t_bd = tmpc[0:64, 1024:1088]; t_mp = tmpc[:, 1088:1344]; t_ms = tmpc[:, 1344:3456]
    dma(t_id, d_ident, [], ['t_id'], key=('c', 't_id'), eng='act')
    cp('dve', ident, t_id, ['t_id'], ['ident'])
    dma(t_ws, d_wsTp, [], ['t_ws'], key=('c', 't_ws'), eng='act')
    dma(t_tri, d_triu, [], ['t_tri'], key=('c', 't_tri'), eng='act')
    tt('dve', WmTp, t_ws.rearrange("p (g t) -> p g t", g=4), t_tri.unsqueeze(1).broadcast_to([128, 4, 128]), ALU.mult,
       ['t_ws', 't_tri'], ['WmTp'])
    dma(t_wss, d_wsTs, [], ['t_wss'], key=('c', 't_wss'), eng='act')
    dma(t_bd, d_bdm, [], ['t_bd'], key=('c', 't_bd'), eng='act')
    tt('dve', WmTs[0:64], t_wss.rearrange("p (g t) -> p g t", g=4), t_bd.unsqueeze(1).broadcast_to([64, 4, 64]), ALU.mult,
       ['t_wss', 't_bd'], ['WmTs'])
    dma(t_mp, d_maskp, [], ['t_mp'], key=('c', 't_mp'), eng='act')
    cp('dve', maskp, t_mp, ['t_mp'], ['maskp'])
    dma(t_ms, d_masks, [], ['t_ms'], key=('c', 't_ms'), eng='act')
    cp('dve', masks, t_ms, ['t_ms'], ['masks'])
    add('pool', lambda e: e.memset(Vaug.rearrange("p b k d -> p (b k d)"), 1.0), [], ['Vaug'])
    add('pool', lambda e: e.memset(Vn.rearrange("p k d -> p (k d)"), 1.0), [], ['Vn'])

    def wview(name):
        w = Wd[name]
        if name.endswith('_up') or name == 'w_in' or name in ('w_pb', 'w_o'):
            return w.rearrange("(kc p) n -> p kc n", p=128)
        return w.rearrange("(j p) n -> p j n", p=128)

    def unit_list(kind):
        groups = [0]
        L = []
        for f in ('ffn1', 'ffn2'):
            sub = []
            up = wview(f + '_up'); dn = wview(f + '_down')
            for i in range(11):
                sub.append((f + 'g%d' % i, up[:, :, 256 * i:256 * i + 256], (8, 256)))
                sub.append((f + 'u%d' % i, up[:, :, 2816 + 256 * i:2816 + 256 * i + 256], (8, 256)))
            for g in groups:
                for i in range(11):
                    sub.append((f + 'd%d_%d' % (g, i), dn[:, 2 * i:2 * i + 2, :], (2, 1024)))
            if f == 'ffn1':
                L += sub
                wi = wview('w_in')
                for c in range(2):
                    L.append(('au%d' % c, wi[:, :, 256 * c:256 * c + 256], (8, 256)))
                for c in range(2):
                    L.append(('av%d' % c, wi[:, :, 512 + 256 * c:512 + 256 * c + 256], (8, 256)))
                for c in range(4):
                    L.append(('q%d' % c, wi[:, :, 1024 + 256 * c:1024 + 256 * c + 256], (8, 256)))
                L.append(('kv', wi[:, :, 2048:2304], (8, 256)))
                pa = wview('w_pa'); pbw = wview('w_pb'); wo = wview('w_o')
                for n in range(2):
                    L.append(('pa%d' % n, pa[:, :, 512 * n:512 * n + 512], (4, 512)))
                    L.append(('ga%d' % (2 * n), wi[:, :, 2304 + 512 * n:2304 + 512 * n + 256], (8, 256)))
                    L.append(('ga%d' % (2 * n + 1), wi[:, :, 2304 + 512 * n + 256:2304 + 512 * n + 512], (8, 256)))
                    L.append(('pb0_%d' % n, pbw[:, 0:4, 512 * n:512 * n + 512], (4, 512)))
                    L.append(('pb1_%d' % n, pbw[:, 4:8, 512 * n:512 * n + 512], (4, 512)))
                    L.append(('gb%d' % (2 * n), wi[:, :, 3328 + 512 * n:3328 + 512 * n + 256], (8, 256)))
                    L.append(('gb%d' % (2 * n + 1), wi[:, :, 3328 + 512 * n + 256:3328 + 512 * n + 512], (8, 256)))
                for n in range(2):
                    L.append(('wo0_%d' % n, wo[:, 0:4, 512 * n:512 * n + 512], (4, 512)))
                    L.append(('wo1_%d' % n, wo[:, 4:8, 512 * n:512 * n + 512], (4, 512)))
            else:
                L += sub
        return L

    import re as _re

    def canon(name):
        return _re.sub(r"d\d_(\d+)$", r"d_\1", name)

    all_units = []
    blocks = ([('p', b) for b in range(n_pblocks)] if do_prompt else []) + [('s', 0)]
    for bidx, (kind, b) in enumerate(blocks):
        for (name, src, shp) in unit_list(kind):
            all_units.append((name, src, shp, bidx))
    cids = {}
    for (name, src, shp, bidx) in all_units:
        if bidx == 0:
            cids.setdefault(canon(name), len(cids))
    wscr = nc.dram_tensor("wscr", [len(cids), 128, 2048], BF16).ap()

    CONV_SPLIT = False

    class WS:
        cur = 0
        issued = 0
        released = 0
        DMAX = 6
        stg_i = 0

    def w_issue_until():
        while WS.issued < len(all_units) and WS.issued <= WS.cur + WS.DMAX and WS.issued - NR < WS.released:
            k = WS.issued
            name, src, (a, b), bidx = all_units[k]
            ri = k % NR
            ci = cids[canon(name)]
            is_ffn = name.startswith('ffn')
            convert = (bidx == 0) or (bidx == 1 and is_ffn and CONV_SPLIT)
            wb = (bidx == 0 and (not is_ffn or not CONV_SPLIT)) or (bidx == 1 and is_ffn and CONV_SPLIT)
            if convert:
                for hf in range(2):
                    si = WS.stg_i % len(stage); WS.stg_i += 1
                    a2 = a // 2
                    sv = stage[si].rearrange("p (a b) -> p a b", a=a2)
                    dma(sv, src[:, hf * a2:(hf + 1) * a2, :], [], [('stg', si)], key=('stg', si))
                    ce = ('act', 'dve')[WS.stg_i % 2]
                    cp(ce, ring[ri][:, hf * 1024:(hf + 1) * 1024], stage[si], [('stg', si)], [('ring', ri)])
                if wb:
                    dma(wscr[ci], ring[ri], [('ring', ri)], [('scr', ci)], key=('wb', ri), eng='pool')
            else:
                dma(ring[ri], wscr[ci], [('scr', ci)], [('ring', ri)], key=('ring', ri))
            WS.issued += 1

    def w_next(expect):
        k = WS.cur
        name, src, (a, b), bidx = all_units[k]
        assert name == expect, (name, expect)
        WS.cur += 1
        w_issue_until()
        assert WS.issued > k, ("weight stream stalled", name, WS.issued, WS.released)
        return ring[k % NR].rearrange("p (a b) -> p a b", a=a), ('ring', k % NR)

    def w_release(n=1):
        WS.released += n
        w_issue_until()

    def layernorm(yi, R, width, g_ap, b_ap, eps, out_ap, out_keys, heavy='dve', ybuf_ap=None, ykey=None):
        y = ybuf[yi] if ybuf_ap is None else ybuf_ap
        yk = ('y', yi) if ykey is None else ykey
        sk = ('lnst', yi)
        stt_ = lnst[yi]
        nch = width // 512
        for c in range(nch):
            add('dve', (lambda c: lambda e: e.bn_stats(out=stt_[0:R, 6 * c:6 * c + 6], in_=y[0:R, 512 * c:512 * c + 512]))(c), [yk], [sk])
        add('dve', lambda e: e.bn_aggr(out=stt_[0:R, 12:14], in_=stt_[0:R, 0:6 * nch]), [sk], [sk])
        tsc('dve', stt_[0:R, 14:15], stt_[0:R, 13:14], eps, None, ALU.add, None, [sk], [sk])
        tt('pool', stt_[0:R, 15:16], stt_[0:R, 14:15], mhalf[0:R, 0:1], ALU.pow, [sk, 'mhalf'], [sk])
        if heavy == 'dve':
            tsc('dve', y[0:R, 0:width], y[0:R, 0:width], stt_[0:R, 12:13], stt_[0:R, 15:16], ALU.subtract, ALU.mult, [yk, sk], [yk])
            tt('dve', y[0:R, 0:width], y[0:R, 0:width], g_ap, ALU.mult, [yk, 'lnc', 'alnc'], [yk])
            tt('dve', out_ap, y[0:R, 0:width], b_ap, ALU.add, [yk, 'lnc', 'alnc'], out_keys)
        else:
            sk2 = ('lnst2', yi)
            tt('pool', lnst2[yi][0:R, 0:1], stt_[0:R, 12:13], stt_[0:R, 15:16], ALU.mult, [sk], [sk2])
            tsc('pool', lnst2[yi][0:R, 0:1], lnst2[yi][0:R, 0:1], -1.0, None, ALU.mult, None, [sk2], [sk2])
            tsc('pool', y[0:R, 0:width], y[0:R, 0:width], stt_[0:R, 15:16], lnst2[yi][0:R, 0:1], ALU.mult, ALU.add, [yk, sk, sk2], [yk])
            tt('pool', y[0:R, 0:width], y[0:R, 0:width], g_ap, ALU.mult, [yk, 'lnc', 'alnc'], [yk])
            tt('pool', out_ap, y[0:R, 0:width], b_ap, ALU.add, [yk, 'lnc', 'alnc'], out_keys)

    def xb_fill(ti, R, src_ap=None, src_keys=None, cast_eng='act', dram_src=None):
        xi = ti % 4
        if dram_src is not None:
            dma(xb[xi][0:R, :], dram_src, [], [('xb', xi)], key=('xb', xi), eng='pool')
        else:
            cp(cast_eng, xb[xi][0:R, :], src_ap, src_keys, [('xb', xi)])

    def xb_transpose(ti, R, c0):
        xi = ti % 4
        b = nb()
        for kc in range(8):
            tr(pbb(b)[:, kc * 128:kc * 128 + R], xb[xi][0:R, kc * 128:(kc + 1) * 128], ident[0:R, 0:R],
               [('xb', xi), 'ident'], [('ps', b)])
        cp('act', actT[:, :, c0:c0 + R], pbb(b).rearrange("p (k t) -> p k t", k=8)[:, :, 0:R], [('ps', b)], [('actT', ti)])

    def to_actT(src_ap, src_keys, R, c0, ti, cast_eng='act'):
        xb_fill(ti, R, src_ap, src_keys, cast_eng)
        xb_transpose(ti, R, c0)

    def run_pipeline(NT, stages):
        minlag = min(l for l, _ in stages)
        stages = [(l - minlag, f) for l, f in stages]
        maxlag = max(l for l, _ in stages)
        order = sorted(stages, key=lambda t: -t[0])
        for step in range(NT + maxlag):
            for lag, fn in order:
                if 0 <= step - lag < NT:
                    fn(step - lag)

    def ln_res_pipeline(tiles, ln_g, ln_b, eps, pre=None, extra=None, ln_i=0):
        NT = len(tiles)

        def s1(ti):
            R = tiles[ti]; rk = ('res', ti); sk = ('lnst', ti % 4); st_ = lnst[ti % 4]
            if pre is not None:
                pre(ti)
            y = res[:, ti, :]
            for c in range(2):
                add('dve', (lambda c=c: lambda e: e.bn_stats(out=st_[0:R, 6 * c:6 * c + 6], in_=y[0:R, 512 * c:512 * c + 512]))(), [rk], [sk])
            add('dve', lambda e: e.bn_aggr(out=st_[0:R, 12:14], in_=st_[0:R, 0:12]), [sk], [sk])
            tsc('dve', st_[0:R, 14:15], st_[0:R, 13:14], eps, None, ALU.add, None, [sk], [sk])
            tt('pool', st_[0:R, 15:16], st_[0:R, 14:15], mhalf[0:R, 0:1], ALU.pow, [sk, 'mhalf'], [sk])

        def s2(ti):
            R = tiles[ti]; rk = ('res', ti); sk = ('lnst', ti % 4); st_ = lnst[ti % 4]
            y = res[0:R, ti, :]
            tsc('dve', y, y, st_[0:R, 12:13], st_[0:R, 15:16], ALU.subtract, ALU.mult, [rk, sk], [rk])

        def s3(ti):
            R = tiles[ti]; rk = ('res', ti); c0 = 128 * ti
            bA = nb(); bB = nb()
            for kc in range(8):
                bk = bA if kc < 4 else bB
                tr(pb(bk)[:, (kc % 4) * 128:(kc % 4) * 128 + R], res[0:R, ti, kc * 128:(kc + 1) * 128], ident32[0:R, 0:R],
                   [rk, 'ident32'], [('ps', bk)])
            for kc in range(8):
                bk = bA if kc < 4 else bB
                actf(actT[:, kc, c0:c0 + R], pb(bk)[:, (kc % 4) * 128:(kc % 4) * 128 + R], AF.Identity, [('ps', bk), 'lncol'], [('actT', ti)],
                     bias=lncol[:, ln_i + 1, kc:kc + 1], scale=lncol[:, ln_i, kc:kc + 1])
            y = res[0:R, ti, :]
            tt('pool', y, y, ln_g[0:R, :], ALU.mult, [rk, 'lnc'], [rk])
            tt('pool', y, y, ln_b[0:R, :], ALU.add, [rk, 'lnc'], [rk])

        run_pipeline(NT, [(0, s1), (1, s2), (2, s3)] + list(extra or []))

    pending_epi = [None]

    def ffn(f, kind, tiles, T, ln_idx, final, out_rows, mid_hook=None):
        NT = len(tiles)
        actT_keys = [('actT', ti) for ti in range(NT)]
        for i in range(11):
            ug, kg = w_next(f + 'g%d' % i)
            uu, ku = w_next(f + 'u%d' % i)
            for jj in range(2):
                j = 2 * i + jj
                bg = nb()
                for kc in range(8):
                    mm(pb(bg)[:, 0:T], ug[:, kc, jj * 128:(jj + 1) * 128], actT[:, kc, 0:T], kc == 0, kc == 7,
                       [kg] + actT_keys, [('ps', bg)])
                bu = nb()
                for kc in range(8):
                    mm(pb(bu)[:, 0:T], uu[:, kc, jj * 128:(jj + 1) * 128], actT[:, kc, 0:T], kc == 0, kc == 7,
                       [ku] + actT_keys, [('ps', bu)])
                sq = j % 2
                actf(sg[sq][:, 0:T], pb(bg)[:, 0:T], AF.Silu, [('ps', bg)], [('sg', sq)])
                tt('dve', hT[:, j, 0:T], sg[sq][:, 0:T], pb(bu)[:, 0:T], ALU.mult, [('sg', sq), ('ps', bu)], [('hT', j)])
            w_release(2)
        if mid_hook is not None:
            mid_hook()
        groups = [list(range(NT))]
        for gi, grp in enumerate(groups):
            banks = {ti: (nb(), nb()) for ti in grp}
            for i in range(11):
                ud, kd = w_next(f + 'd%d_%d' % (gi, i))
                for jj in range(2):
                    j = 2 * i + jj
                    for ti in grp:
                        R = tiles[ti]; c0 = 128 * ti
                        for n in range(2):
                            mm(pb(banks[ti][n])[0:R, :], hT[:, j, c0:c0 + R], ud[:, jj, n * 512:(n + 1) * 512],
                               j == 0, j == NJ - 1, [kd, ('hT', j)], [('ps', banks[ti][n])])
                w_release(1)
            def epilogue(extra=None, part=0, grp=grp, banks=banks):
              def pre(ti):
                  R = tiles[ti]; rk = ('res', ti)
                  for n in range(2):
                      stt('dve', res[0:R, ti, n * 512:(n + 1) * 512], res[0:R, ti, n * 512:(n + 1) * 512], 2.0 * ALPHA,
                          pb(banks[ti][n])[0:R, :], ALU.mult, ALU.add, [rk, ('ps', banks[ti][n])], [rk])
              if not final:
                  ln_res_pipeline(tiles, lnc[:, 2 * ln_idx, :], lnc[:, 2 * ln_idx + 1, :], 4.0 * LN_EPS, pre=pre, extra=extra, ln_i=2 * ln_idx)
              else:
                  if part in (0, 1):
                      for ti in grp:
                          pre(ti)
                  if part in (0, 2):
                      for ti in grp:
                          R = tiles[ti]; rk = ('res', ti); yi = ti % 4
                          layernorm(yi, R, 1024, lnc[0:R, 2 * ln_idx, :], lnc[0:R, 2 * ln_idx + 1, :], 4.0 * LN_EPS,
                                    res[0:R, ti, :], [rk], heavy='pool', ybuf_ap=res[:, ti, :], ykey=rk)
                          dma(out_rows(ti, R), res[0:R, ti, :], [rk], [], key=('yout', yi), eng='pool')
            return epilogue

    def run_block(kind, bi):
        if kind == 's':
            tiles = [64]
        else:
            tiles = [128] * 4
        NT = len(tiles); T = sum(tiles)
        actT_keys = [('actT', ti) for ti in range(NT)]

        def x_rows(ti, R):
            if kind == 's':
                return x_s[0:64, :]
            r0 = bi * 512 + ti * 128
            return x_p[r0:r0 + 128, :]

        def y_rows(ti, R):
            if kind == 's':
                return y_s[0:64, :]
            r0 = bi * 512 + ti * 128
            return y_p[r0:r0 + 128, :]

        if kind == 's' or bi == 0:
            cur_extra[0] = S.frontier()
        for ti, R in enumerate(tiles):
            xb_fill(ti, R, dram_src=x_rows(ti, R))
        if pending_epi[0] is not None:
            pending_epi[0](part=1)
        for ti, R in enumerate(tiles):
            xb_transpose(ti, R, 128 * ti)
        if pending_epi[0] is not None:
            pending_epi[0](part=2)
            pending_epi[0] = None
        if kind == 's':
            dma(ckb.rearrange("p (b d) -> p b d", b=16), ck.rearrange("b s d -> s b d"), [], ['ckb'], key='ckb', eng='pool')
            for b4 in range(4):
                bk = nb()
                for q in range(8):
                    bb = b4 * 4 + q // 2; kvh = q % 2
                    tr(pbb(bk)[0:64, q * 128:(q + 1) * 128], ckb[:, bb * 128 + kvh * 64:bb * 128 + kvh * 64 + 64], ident,
                       ['ckb', 'ident'], [('ps', bk)])
                for kvh in range(2):
                    src = pbb(bk)[0:64, :].rearrange("p (b k s) -> p b k s", b=4, k=2)[:, :, kvh, :]
                    dst = KT[0:64, kvh, b4 * 512:(b4 + 1) * 512].rearrange("p (b s) -> p b s", b=4)
                    cp('dve', dst, src, [('ps', bk)], ['KT'])
            for kvh in range(2):
                dma(Vaug[:, :, kvh, 0:64], cv.rearrange("b s (k d) -> s b k d", k=2)[:, :, kvh, :], [], ['Vaug'], key='vaug', eng='pool')
            dma(wks[:, 0:124, :], ck[:, 4:128, :], [], [], key='wk1')
            dma(wvs[:, 0:124, :], cv[:, 4:128, :], [], [], key='wv1')
        def load_res():
            for ti, R in enumerate(tiles):
                dma(res[0:R, ti, :], x_rows(ti, R), [], [('res', ti)], key=('res', ti))

        epi1 = ffn('ffn1', kind, tiles, T, 0, False, None, mid_hook=load_res)

        cur_extra[0] = S.frontier()
        uau = [w_next('au0'), w_next('au1')]
        uav = [w_next('av0'), w_next('av1')]
        Wm = WmTs if kind == 's' else WmTp
        bia = biass if kind == 's' else biasp

        def D1(ti):
            R = tiles[ti]; c0 = 128 * ti
            b = nb()
            for c in range(2):
                u, k = uav[c]
                for kc in range(8):
                    mm(pb(b)[0:R, c * 256:(c + 1) * 256], actT[:, kc, c0:c0 + R], u[:, kc, :], kc == 0, kc == 7,
                       [k, ('actT', ti)], [('ps', b)])
            actf(ybuf[0][0:R, 0:512], pb(b)[0:R, :], AF.Gelu_apprx_tanh, [('ps', b)], [('y', 0)])

        def D2(ti):
            R = tiles[ti]; tg = bi * 4 + ti
            layernorm(4, R, 512, alnc[0:R, 0, :], alnc[0:R, 1, :], LN_EPS, vn[0:R, :], ['vn'], ybuf_ap=ybuf[0], ykey=('y', 0))
            if kind == 's':
                dma(cvs[0:64, :], vn[0:64, :], ['vn'], [], key='cvout')
            elif tg == 15:
                dma(cvp[:, :], vn[:, :], ['vn'], [], key='cvout')
            cp('act', vnb[0:R, :], vn[0:R, :], ['vn'], ['vnb'])

        def D3(ti):
            R = tiles[ti]; c0 = 128 * ti
            for g in range(4):
                u, k = uau[g // 2]
                b = nb()
                for kc in range(8):
                    mm(pb(b)[:, 0:R], u[:, kc, (g % 2) * 128:(g % 2) * 128 + 128], actT[:, kc, c0:c0 + R], kc == 0, kc == 7,
                       [k, ('actT', ti)], [('ps', b)])
                actf(guT[:, g, 0:R], pb(b)[:, 0:R], AF.Gelu_apprx_tanh, [('ps', b)], ['guT'])
            b = nb()
            for g in range(4):
                mm(pb(b)[:, g * 128:g * 128 + R], vnb[0:R, g * 128:(g + 1) * 128], Wm[0:R, g, 0:R], True, True,
                   ['vnb', 'WmTp', 'WmTs'], [('ps', b)])
            tt('dve', mixtmp[:, :, 0:R], pb(b).rearrange("p (g t) -> p g t", g=4)[:, :, 0:R], bia[:, :, 0:R], ALU.add,
               [('ps', b), 'biasp', 'biass'], ['mixtmp'])
            tt('pool', a_outT[:, :, c0:c0 + R], mixtmp[:, :, 0:R], guT[:, :, 0:R], ALU.mult, ['mixtmp', 'guT'], [('a_outT', ti)])

        epi1(extra=[(3, D1), (4, D2), (5, D3)])
        w_release(4)

        uq = [w_next('q%d' % c) for c in range(4)]
        ukv = w_next('kv')
        def make_E(ti):
            R = tiles[ti]
            kcol = 2048 if kind == 's' else (bi * 4 + ti) * 128
            c0 = 128 * ti
            tg = bi * 4 + ti
            if kind == 's':
                cosb = lambda n: ropes[0:R, 0, :].unsqueeze(1).broadcast_to([R, n, 32])
                sinb = lambda n: ropes[0:R, 1, :].unsqueeze(1).broadcast_to([R, n, 32])
            else:
                cosb = lambda n: ropep[0:R, 0, tg, :].unsqueeze(1).broadcast_to([R, n, 32])
                sinb = lambda n: ropep[0:R, 1, tg, :].unsqueeze(1).broadcast_to([R, n, 32])

            def rope(src_bank, nh, dst1, dst2, dkeys):
                xv = pb(src_bank)[0:R, 0:nh * 64].rearrange("p (h c f) -> p h c f", h=nh, c=2)
                x1 = xv[:, :, 0, :]; x2 = xv[:, :, 1, :]
                r = [rt[i][0:R, 0:nh * 32].rearrange("p (h f) -> p h f", h=nh) for i in range(4)]
                rk = ['ropep', 'ropes', ('ps', src_bank)]
                tt('dve', r[0], x1, cosb(nh), ALU.mult, rk, [('rt', 0)])
                tt('dve', r[1], x2, sinb(nh), ALU.mult, rk, [('rt', 1)])
                tt('pool', dst1, r[0], r[1], ALU.subtract, [('rt', 0), ('rt', 1)], dkeys)
                tt('dve', r[2], x2, cosb(nh), ALU.mult, rk, [('rt', 2)])
                tt('dve', r[3], x1, sinb(nh), ALU.mult, rk, [('rt', 3)])
                tt('pool', dst2, r[2], r[3], ALU.add, [('rt', 2), ('rt', 3)], dkeys)

            def E1():
                for c in range(4):
                    u, k = uq[c]
                    b = nb()
                    for kc in range(8):
                        mm(pb(b)[0:R, 0:256], actT[:, kc, c0:c0 + R], u[:, kc, :], kc == 0, kc == 7, [k, ('actT', ti)], [('ps', b)])
                    rope(b, 4, q_rot[0:R, 4 * c:4 * c + 4, 0:32], q_rot[0:R, 4 * c:4 * c + 4, 32:64], ['q_rot'])
                u, k = ukv
                b = nb()
                for kc in range(8):
                    mm(pb(b)[0:R, 0:256], actT[:, kc, c0:c0 + R], u[:, kc, :], kc == 0, kc == 7, [k, ('actT', ti)], [('ps', b)])
                kf3 = kf[0:R, :].rearrange("p (h d) -> p h d", h=2)
                rope(b, 2, kf3[:, :, 0:32], kf3[:, :, 32:64], ['kf'])
                cp('act', vf[0:R, :], pb(b)[0:R, 128:256], [('ps', b)], ['vf'])
                cp('act', kbf[0:R, :], kf[0:R, :], ['kf'], ['kbf'])
                if kind == 's':
                    cp('dve', Vn[0:R, :, 0:64], vf[0:R, :].rearrange("p (k d) -> p k d", k=2), ['vf'], ['Vn'])
                    kcol = 2048
                    for b_ in range(SB):
                        dma(wks[b_, 124:128, :], kf[4 * b_:4 * b_ + 4, :], ['kf'], [], key='wk2')
                        dma(wvs[b_, 124:128, :], vf[4 * b_:4 * b_ + 4, :], ['vf'], [], key='wv2')
                else:
                    cp('dve', Vaug[0:R, tg, :, 0:64], vf[0:R, :].rearrange("p (k d) -> p k d", k=2), ['vf'], [('Vaug', tg)])
                    kcol = tg * 128
                    if tg == 15:
                        dma(wkp[:, :], kf[:, :], ['kf'], [], key='wk2')
                        dma(wvp[:, :], vf[:, :], ['vf'], [], key='wv2')

            def E2():
                b1 = nb(); b2 = nb()
                for h in range(16):
                    bq = b1 if h < 8 else b2
                    tr(pbb(bq)[0:64, (h % 8) * 128:(h % 8) * 128 + R], q_rot[0:R, h, :], ident[0:R, 0:R], ['q_rot', 'ident'], [('ps', bq)])
                cp('act', (qTs if kind == 's' else qT)[0:64, 0:8, 0:R], pbb(b1)[0:64, :].rearrange("p (h t) -> p h t", h=8)[:, :, 0:R], [('ps', b1)], ['qT'])
                cp('act', (qTs if kind == 's' else qT)[0:64, 8:16, 0:R], pbb(b2)[0:64, :].rearrange("p (h t) -> p h t", h=8)[:, :, 0:R], [('ps', b2)], ['qT'])
                b3 = nb()
                for kvh in range(2):
                    tr(pbb(b3)[0:64, kvh * 128:kvh * 128 + R], kbf[0:R, kvh * 64:(kvh + 1) * 64], ident[0:R, 0:R], ['kbf', 'ident'], [('ps', b3)])
                kt_key = 'KT' if kind == 's' else ('KT', tg)
                cp('act', KT[0:64, :, kcol:kcol + R], pbb(b3)[0:64, 0:256].rearrange("p (k t) -> p k t", k=2)[:, :, 0:R], [('ps', b3)], [kt_key])


            def E3():
                if kind == 'p':
                    if tg == 0:
                        blks = [0]; k0 = 0; nk = 128; mk = maskp[:, 128:256]
                    else:
                        blks = [tg - 1, tg]; k0 = (tg - 1) * 128; nk = 256; mk = maskp[:, 0:256]
                    ktk = [('KT', t_) for t_ in blks]
                    vk = [('Vaug', t_) for t_ in blks]
                    nbk = len(blks)
                    stt8 = {}

                    def att_st0(hp):
                        pi = hp % 3
                        b = nb()
                        c2 = slice(2 * hp, 2 * hp + 2)
                        for hh in range(2):
                            h = 2 * hp + hh; kvh = h // 8
                            mm(pb(b)[0:R, hh * 256:hh * 256 + nk], qT[0:64, h, 0:R], KT[0:64, kvh, k0:k0 + nk], True, False,
                               ['qT'] + ktk, [('ps', b)])
                            mm(pb(b)[0:R, hh * 256:hh * 256 + nk], ident[0:R, 0:R], mk[0:R, :], False, True, ['ident', 'maskp'], [('ps', b)])
                        sv = pb(b)[0:R, :].rearrange("p (h k) -> p h k", h=2)[:, :, 0:nk]
                        add('dve', (lambda sv=sv: lambda e: e.tensor_reduce(out=sm['mx'][0:R, c2], in_=sv, axis=AX.X, op=ALU.max))(), [('ps', b)], [('mx', hp)])
                        stt('dve', sm['negm'][0:R, c2], sm['mx'][0:R, c2], -SCALE, negsinks[0:R, c2], ALU.mult, ALU.min,
                            [('mx', hp), 'negsinks'], [('negm', hp)])
                        tt('dve', sm['tmp2'][0:R, c2], sinks[0:R, c2], sm['negm'][0:R, c2], ALU.add, [('negm', hp), 'sinks'], [('tmp2', hp)])
                        actf(sm['es'][0:R, c2], sm['tmp2'][0:R, c2], AF.Exp, [('tmp2', hp)], [('es', hp)])
                        P3 = Pb[pi][0:R, :].rearrange("p (h k) -> p h k", h=2)
                        for hh in range(2):
                            actf(P3[:, hh, 0:nk], pb(b)[0:R, hh * 256:hh * 256 + nk], AF.Exp, [('ps', b), ('negm', hp)], [('P', pi)],
                                 bias=sm['negm'][0:R, 2 * hp + hh:2 * hp + hh + 1], scale=SCALE)

                    def att_st1(hp):
                        pi = hp % 3
                        p3i = hp % 3
                        P3 = Pb[p3i][0:R, :].rearrange("p (h k) -> p h k", h=2)
                        bt = nb()
                        for hh in range(2):
                            for bj in range(nbk):
                                sl = hh * 2 + bj
                                tr(pbb(bt)[:, sl * 128:sl * 128 + R], P3[:, hh, bj * 128:(bj + 1) * 128], ident[0:R, 0:R], [('P', p3i), 'ident'], [('ps', bt)])
                        PT3 = PT[pi].rearrange("p (s t) -> p s t", s=4)
                        ceng = 'act'
                        if nbk == 2:
                            cp(ceng, PT3[:, :, 0:R], pbb(bt)[:, 0:512].rearrange("p (s t) -> p s t", s=4)[:, :, 0:R], [('ps', bt)], [('PT', pi)])
                        else:
                            src = pbb(bt)[:, 0:512].rearrange("p (h s t) -> p h s t", h=2, s=2)[:, :, 0, 0:R]
                            dst = PT[pi].rearrange("p (h s t) -> p h s t", h=2, s=2)[:, :, 0, 0:R]
                            cp(ceng, dst, src, [('ps', bt)], [('PT', pi)])

                    def att_st2(hp):
                        pi = hp % 3
                        c2 = slice(2 * hp, 2 * hp + 2)
                        PT3 = PT[pi].rearrange("p (s t) -> p s t", s=4)
                        bo = nb()
                        for hh in range(2):
                            h = 2 * hp + hh; kvh = h // 8
                            for bj in range(nbk):
                                sl = hh * 2 + bj
                                mm(pb(bo)[0:R, hh * 128:hh * 128 + 65], PT3[:, sl, 0:R], Vaug[:, blks[bj], kvh, 0:65], bj == 0, bj == nbk - 1,
                                   [('PT', pi), 'Vaug'] + vk, [('ps', bo)])
                        O3 = pb(bo)[0:R, 0:256].rearrange("p (h d) -> p h d", h=2)
                        tt('dve', sm['den'][0:R, c2], O3[:, :, 64], sm['es'][0:R, c2], ALU.add, [('ps', bo), ('es', hp)], [('den', hp)])
                        add('dve', lambda e: e.reciprocal(out=sm['rden'][0:R, c2], in_=sm['den'][0:R, c2]), [('den', hp)], [('rden', hp)])
                        tt('dve', b_out[0:R, 2 * hp:2 * hp + 2, :], O3[:, :, 0:64],
                           sm['rden'][0:R, c2].unsqueeze(2).broadcast_to([R, 2, 64]), ALU.mult, [('ps', bo), ('rden', hp)], ['b_out'])

                    for step in range(8 + 4):
                        if step < 8:
                            att_st0(step)
                        if 0 <= step - 2 < 8:
                            att_st1(step - 2)
                        if 0 <= step - 4 < 8:
                            att_st2(step - 4)
                else:
                    widths = [512, 512, 512, 512, 64]
                    sst = {}

                    def sa0(hp):
                        kvh = hp // 4
                        lhs2 = qT_raw[0:64, 128 * hp:128 * hp + 128]
                        sbk = []
                        for c in range(5):
                            w = widths[c]
                            b = nb(); sbk.append(b)
                            mm(pb(b)[:, 0:w], lhs2, KT[0:64, kvh, c * 512:c * 512 + w], True, False, ['qT', 'KT'], [('ps', b)])
                            mm(pb(b)[:, 0:w], ident, masks[:, c * 512:c * 512 + w], False, True, ['ident', 'masks'], [('ps', b)])
                            add('dve', (lambda b=b, w=w, c=c, hp=hp: lambda e: e.tensor_reduce(out=mx5s[hp % 2][:, c:c + 1], in_=pb(b)[:, 0:w], axis=AX.X, op=ALU.max))(),
                                [('ps', b)], [('mx5', hp % 2)])
                        c1 = slice(hp, hp + 1)
                        add('dve', (lambda hp=hp, c1=c1: lambda e: e.tensor_reduce(out=sm['mx'][:, c1], in_=mx5s[hp % 2][:, 0:5], axis=AX.X, op=ALU.max))(),
                            [('mx5', hp % 2)], [('mx', hp)])
                        stt('dve', sm['negm'][:, c1], sm['mx'][:, c1], -SCALE, negsinks2[:, c1], ALU.mult, ALU.min, [('mx', hp), 'negsinks2'], [('negm', hp)])
                        tt('dve', sm['tmp2'][:, c1], sinks2[:, c1], sm['negm'][:, c1], ALU.add, [('negm', hp), 'sinks2'], [('tmp2', hp)])
                        actf(sm['es'][:, c1], sm['tmp2'][:, c1], AF.Exp, [('tmp2', hp)], [('es', hp)])
                        sst[hp] = sbk

                    def sa0b(hp):
                        c1 = slice(hp, hp + 1)
                        sbk = sst[hp]
                        for c in range(5):
                            w = widths[c]
                            actf(Ps[:, c * 512:c * 512 + w], pb(sbk[c])[:, 0:w], AF.Exp, [('ps', sbk[c]), ('negm', hp)], ['Ps'],
                                 bias=sm['negm'][:, c1], scale=SCALE)

                    def sa1(hp):
                        bts = [nb(), nb(), nb()]
                        for bj in range(17):
                            kw = 128 if bj < 16 else 64
                            bt = bts[bj // 8]
                            sl = bj % 8
                            tr(pbb(bt)[0:kw, sl * 128:(sl + 1) * 128], Ps[:, bj * 128:bj * 128 + kw], ident, ['Ps', 'ident'], [('ps', bt)])
                        cp('dve', PTs[:, 0:8, :], pbb(bts[0]).rearrange("p (s t) -> p s t", s=8), [('ps', bts[0])], ['PTs'])
                        cp('act', PTs[:, 8:16, :], pbb(bts[1]).rearrange("p (s t) -> p s t", s=8), [('ps', bts[1])], ['PTs'])
                        cp('dve', PTs[0:64, 16, :], pbb(bts[2])[0:64, 0:128], [('ps', bts[2])], ['PTs'])

                    def sa2(hp):
                        kvh = hp // 4
                        c1 = slice(hp, hp + 1)
                        bo = nb()
                        for bj in range(17):
                            if bj < 16:
                                mm(pb(bo)[:, 0:65], PTs[:, bj, :], Vaug[:, bj, kvh, 0:65], bj == 0, False, ['PTs', 'Vaug'], [('ps', bo)])
                            else:
                                mm(pb(bo)[:, 0:65], PTs[0:64, 16, :], Vn[0:64, kvh, 0:65], False, True, ['PTs', 'Vn'], [('ps', bo)])
                        tt('dve', sm['den'][:, c1], pb(bo)[:, 64:65], sm['es'][:, c1], ALU.add, [('ps', bo), ('es', hp)], [('den', hp)])
                        add('dve', (lambda c1=c1: lambda e: e.reciprocal(out=sm['rden'][:, c1], in_=sm['den'][:, c1]))(), [('den', hp)], [('rden', hp)])
                        tsc('dve', tmpO, pb(bo)[:, 0:64], sm['rden'][:, c1], None, ALU.mult, None, [('ps', bo), ('rden', hp)], ['tmpO'])
                        cp('dve', b_out[0:64, 2 * hp, :], tmpO[0:64, :], ['tmpO'], ['b_out'])
                        cp('dve', b_out[0:64, 2 * hp + 1, :], tmpO[64:128, :], ['tmpO'], ['b_out'])

                    sa0(0)
                    sa0b(0)
                    for hp in range(8):
                        if hp + 1 < 8:
                            sa0(hp + 1)
                        sa1(hp)
                        if hp + 1 < 8:
                            sa0b(hp + 1)
                        sa2(hp)
            def E4():
                b = nb()
                bo2 = b_out[0:R].rearrange("p h d -> p (h d)")
                for kc in range(8):
                    tr(pbb(b)[:, kc * 128:kc * 128 + R], bo2[:, kc * 128:(kc + 1) * 128], ident[0:R, 0:R], ['b_out', 'ident'], [('ps', b)])
                cp('act', b_outT[:, :, c0:c0 + R], pbb(b).rearrange("p (k t) -> p k t", k=8)[:, :, 0:R], [('ps', b)], [('b_outT', ti)])

            return (E1, E2, E3, E4)

        Es = [make_E(ti) for ti in range(NT)]
        run_pipeline(NT, [(0, lambda t: Es[t][0]()), (1, lambda t: Es[t][1]()), (2, lambda t: Es[t][2]()), (3, lambda t: Es[t][3]())])
        w_release(5)

        cur_extra[0] = S.frontier()
        for n in range(2):
            upa = w_next('pa%d' % n); uga = [w_next('ga%d' % (2 * n)), w_next('ga%d' % (2 * n + 1))]
            for ti, R in enumerate(tiles):
                c0 = 128 * ti
                ba = nb()
                for g in range(4):
                    mm(pb(ba)[0:R, :], a_outT[:, g, c0:c0 + R], upa[0][:, g, :], g == 0, g == 3, [upa[1], ('a_outT', ti)], [('ps', ba)])
                bg = nb()
                for c in range(2):
                    for kc in range(8):
                        mm(pb(bg)[0:R, c * 256:(c + 1) * 256], actT[:, kc, c0:c0 + R], uga[c][0][:, kc, :], kc == 0, kc == 7,
                           [uga[c][1], ('actT', ti)], [('ps', bg)])
                si = ti % 2
                actf(sig[si][0:R, :], pb(bg)[0:R, :], AF.Sigmoid, [('ps', bg)], [('sig', si)])
                tt('dve', t1[0:R, ti, :], sig[si][0:R, :], pb(ba)[0:R, :], ALU.mult, [('sig', si), ('ps', ba)], [('t1', ti)])
            w_release(3)
            upb = [w_next('pb0_%d' % n), w_next('pb1_%d' % n)]
            ugb = [w_next('gb%d' % (2 * n)), w_next('gb%d' % (2 * n + 1))]
            for ti, R in enumerate(tiles):
                c0 = 128 * ti
                bb_ = nb()
                for kc in range(8):
                    mm(pb(bb_)[0:R, :], b_outT[:, kc, c0:c0 + R], upb[kc // 4][0][:, kc % 4, :], kc == 0, kc == 7,
                       [upb[kc // 4][1], ('b_outT', ti)], [('ps', bb_)])
                bg = nb()
                for c in range(2):
                    for kc in range(8):
                        mm(pb(bg)[0:R, c * 256:(c + 1) * 256], actT[:, kc, c0:c0 + R], ugb[c][0][:, kc, :], kc == 0, kc == 7,
                           [ugb[c][1], ('actT', ti)], [('ps', bg)])
                si = ti % 2
                actf(sig[si][0:R, :], pb(bg)[0:R, :], AF.Sigmoid, [('ps', bg)], [('sig', si)])
                tt('dve', t2[0:R, :], sig[si][0:R, :], pb(bb_)[0:R, :], ALU.mult, [('sig', si), ('ps', bb_)], ['t2'])
                tt('pool', mergedb[0:R, ti, n * 512:(n + 1) * 512], t2[0:R, :], t1[0:R, ti, :], ALU.add, ['t2', ('t1', ti)], [('mergedb', ti)])
            w_release(4)
        uwo = [[w_next('wo0_0'), w_next('wo1_0')], [w_next('wo0_1'), w_next('wo1_1')]]

        def M1(ti):
            R = tiles[ti]; c0 = 128 * ti
            b = nb()
            for kc in range(8):
                tr(pbb(b)[:, kc * 128:kc * 128 + R], mergedb[0:R, ti, kc * 128:(kc + 1) * 128], ident[0:R, 0:R], [('mergedb', ti), 'ident'], [('ps', b)])
            cp('act', actT[:, :, c0:c0 + R], pbb(b).rearrange("p (k t) -> p k t", k=8)[:, :, 0:R], [('ps', b)], [('actT', ti)])

        def M2(ti):
            R = tiles[ti]; c0 = 128 * ti
            rk = ('res', ti)
            bks = []
            for n in range(2):
                b = nb(); bks.append(b)
                for kc in range(8):
                    u, k = uwo[n][kc // 4]
                    mm(pb(b)[0:R, :], actT[:, kc, c0:c0 + R], u[:, kc % 4, :], kc == 0, kc == 7, [k, ('actT', ti)], [('ps', b)])
            for n in range(2):
                stt('dve', res[0:R, ti, n * 512:(n + 1) * 512], res[0:R, ti, n * 512:(n + 1) * 512], ALPHA,
                    pb(bks[n])[0:R, :], ALU.mult, ALU.add, [rk, ('ps', bks[n])], [rk])

        ln_res_pipeline(tiles, lnc[:, 2, :], lnc[:, 3, :], LN_EPS, extra=[(-2, M1), (-1, M2)], ln_i=2)
        w_release(4)
        cur_extra[0] = S.frontier()
        pending_epi[0] = ffn('ffn2', kind, tiles, T, 2, True, y_rows)

    for kind, b in blocks:
        run_block(kind, b)
    pending_epi[0]()
    assert WS.cur == len(all_units), (WS.cur, len(all_units))
    S.emit(nc)
    st.close()
    return nc


_PROG = {}


def _consts():
    c = {}
    c['ident'] = np.eye(128, dtype=np.float32)
    s = np.arange(128)
    c['triu'] = (s[:, None] <= s[None, :]).astype(np.float32)
    bs = np.arange(64)
    c['bdm'] = (((bs[:, None] // 4) == (bs[None, :] // 4)) & ((bs[:, None] % 4) <= (bs[None, :] % 4))).astype(np.float32)
    inv = 10000.0 ** (-(np.arange(32, dtype=np.float64) / 32.0))
    p = np.arange(128)
    t = np.arange(16)
    pos = (t[None, :] * 128 + p[:, None]).astype(np.float64)
    ang = pos[:, :, None] * inv[None, None, :]
    rp = np.stack([np.cos(ang), np.sin(ang)], axis=1)
    c['rope_p'] = np.ascontiguousarray(rp.reshape(128, 1024)).astype(np.float32)
    pos_s = (PAST_LEN + (np.arange(64) % 4)).astype(np.float64)
    ang_s = pos_s[:, None] * inv[None, :]
    rs = np.stack([np.cos(ang_s), np.sin(ang_s)], axis=1)
    c['rope_s'] = np.ascontiguousarray(rs.reshape(64, 64)).astype(np.float32)
    tq = np.arange(128)[:, None]; sk = np.arange(128)[None, :]
    mprev = np.where(sk > tq, 0.0, NEG); mcur = np.where(sk <= tq, 0.0, NEG)
    c['mask_p'] = np.concatenate([mprev, mcur], axis=1).astype(np.float32)
    q = np.arange(64); qb = q // 4; qt = q % 4
    col = np.arange(2048); cb = col // 128; cs = col % 128
    m1 = np.where((cb[None, :] == qb[:, None]) & (cs[None, :] >= qt[:, None] + 1), 0.0, NEG)
    coln = np.arange(64); nb_ = coln // 4; nt = coln % 4
    m2 = np.where((nb_[None, :] == qb[:, None]) & (nt[None, :] <= qt[:, None]), 0.0, NEG)
    ms = np.concatenate([m1, m2], axis=1).astype(np.float32)
    c['mask_s'] = np.ascontiguousarray(np.concatenate([ms, ms], axis=0))
    return c


def kernel(x_prompt, x_sample, cache_win_k, cache_win_v, ffn1_up, ffn1_down, ln1_g, ln1_b,
           w_in, a_ln_g, a_ln_b, a_ws, a_bs, attn_sinks, w_pa, w_pb, w_o, ln2_g, ln2_b,
           ffn2_up, ffn2_down, ln3_g, ln3_b):
    f = lambda a: np.ascontiguousarray(np.asarray(a, dtype=np.float32))
    if 'nc' not in _PROG:
        _PROG['nc'] = build_program()
    nc = _PROG['nc']
    shared = dict(
        ffn1_up=f(ffn1_up[0]), ffn1_down=f(ffn1_down[0]), w_in=f(w_in[0]), w_pa=f(w_pa[0]), w_pb=f(w_pb[0]),
        w_o=f(w_o[0]), ffn2_up=f(ffn2_up[0]), ffn2_down=f(ffn2_down[0]))
    lnrow = np.concatenate([np.asarray(v[0]) for v in (ln1_g, ln1_b, ln2_g, ln2_b, ln3_g, ln3_b)])
    shared['lnc'] = f(np.broadcast_to(lnrow[None, :], (128, 6144)))
    shared['lncol'] = f(np.transpose(lnrow.reshape(6, 8, 128), (2, 0, 1)).reshape(128, 48))
    arow = np.concatenate([np.asarray(a_ln_g[0]), np.asarray(a_ln_b[0])])
    shared['alnc'] = f(np.broadcast_to(arow[None, :], (128, 1024)))
    ws = np.asarray(a_ws[0])
    shared['wsT_p'] = f(np.transpose(ws, (2, 0, 1)).reshape(128, 512))
    bsv = np.asarray(a_bs[0])
    shared['bias_p'] = f(np.broadcast_to(bsv[None], (128, 4, 128)).reshape(128, 512))
    subT = np.transpose(ws[:, 0:4, 0:4], (2, 0, 1))
    shared['wsT_s'] = f(np.tile(subT[None, :, :, None, :], (16, 1, 1, 16, 1)).reshape(64, 256))
    shared['bias_s'] = f(np.broadcast_to(np.tile(bsv[:, 0:4], (1, 16))[None], (128, 4, 64)).reshape(128, 256))
    shared['sinks'] = f(np.broadcast_to(np.asarray(attn_sinks[0])[None, :], (128, 16)))
    shared['sinks2'] = f(np.repeat(np.asarray(attn_sinks[0]).reshape(8, 2).T, 64, axis=0))
    shared.update(_consts())
    xp = np.asarray(x_prompt); xs = np.asarray(x_sample)
    ckf = np.asarray(cache_win_k); cvf = np.asarray(cache_win_v)
    in_maps = []
    for c in range(NCORES):
        m = dict(shared)
        m['x_p'] = f(xp[c])
        m['x_s'] = f(xs[SB * c:SB * (c + 1)].reshape(64, 1024))
        m['ck'] = f(ckf[0, SB * c:SB * (c + 1)].reshape(SB, 128, 128))
        m['cv'] = f(cvf[0, SB * c:SB * (c + 1)].reshape(SB, 128, 128))
        in_maps.append(m)
    res = run_bass_kernel_spmd(nc, in_maps, core_ids=list(range(NCORES)))
    R = res.results
    y_prompt = np.stack([R[c]['y_p'] for c in range(NCORES)]).astype(np.float32)
    y_sample = np.concatenate([R[c]['y_s'].reshape(SB, 4, 1024) for c in range(NCORES)]).astype(np.float32)
    wkp = np.stack([R[c]['wkp'].reshape(128, 2, 64) for c in range(NCORES)])[None].astype(np.float32)
    wvp = np.stack([R[c]['wvp'].reshape(128, 2, 64) for c in range(NCORES)])[None].astype(np.float32)
    wks = np.concatenate([R[c]['wks'].reshape(SB, 128, 2, 64) for c in range(NCORES)])[None].astype(np.float32)
    wvs = np.concatenate([R[c]['wvs'].reshape(SB, 128, 2, 64) for c in range(NCORES)])[None].astype(np.float32)
    cvp = np.stack([R[c]['cvp'] for c in range(NCORES)])[None].astype(np.float32)
    cvs = np.concatenate([R[c]['cvs'].reshape(SB, 4, 512) for c in range(NCORES)])[None].astype(np.float32)
    return (y_prompt, y_sample, wkp, wvp, wks, wvs, cvp, cvs)
```
